# Optimizing a Trainium2 kernel written in Bass

```python
import math
import jax, jax.numpy as jnp
from jax import lax
import numpy as np

D_MODEL = 1024
BATCH = 8
SEQ = 2048
DEPTH = 2
DEC_BATCH = 128
DEC_SEQ = 1
PAST_LEN = 16384
PAGE_SIZE = 128

N_MIXERS = 2
N_CONV_LAYERS = (DEPTH + 1) // 2
N_REC_LAYERS = DEPTH // 2
CONV_WIDTH = 3
REC_HEAD_DIM = 128
REC_HEADS = D_MODEL // REC_HEAD_DIM
CHUNK = 64
N_MEM = 256
X_HEADS = 4
X_HEAD_DIM = D_MODEL // X_HEADS
D_FF = 2816
EPS = 1e-6

kernel_name = "macaron_conv_hgrn2_memxattn_step"


def rmsnorm(x, g):
    xf = x.astype(jnp.float32)
    y = xf * lax.rsqrt(jnp.mean(xf * xf, axis=-1, keepdims=True) + EPS)
    return (y * g.astype(jnp.float32)).astype(x.dtype)


def swiglu(x, w_gate, w_up, w_down):
    return (jax.nn.silu(x @ w_gate) * (x @ w_up)) @ w_down


def short_conv_mixer(x, buf, w_in, w_conv, w_out):
    T = x.shape[1]
    b_gate, c_gate, u = jnp.split(x @ w_in, 3, axis=-1)
    v = c_gate * u
    vv = jnp.concatenate([buf.astype(v.dtype), v], axis=1)
    conv = sum(w_conv[j] * vv[:, j:j + T] for j in range(CONV_WIDTH))
    y = (b_gate * conv) @ w_out
    return y, vv[:, vv.shape[1] - (CONV_WIDTH - 1):]


def hgrn2_lower_bounds(lb_raw):
    p = jax.nn.softmax(lb_raw.astype(jnp.float32), axis=0)
    return jnp.cumsum(p, axis=0) - p[0]


def gla_chunked(q, k, logf, v, S0):
    B, T, H, K = q.shape
    V = v.shape[-1]
    C = min(CHUNK, T)
    n = -(-T // C)
    pad = n * C - T

    def to_chunks(a):
        a = jnp.pad(a, ((0, 0), (0, pad), (0, 0), (0, 0)))
        return a.reshape(B, n, C, H, a.shape[-1]).transpose(1, 0, 2, 3, 4)

    causal = jnp.tril(jnp.ones((C, C), dtype=bool))[None, :, :, None, None]

    def step(S, blk):
        qc, kc, lc, vc = blk
        b = jnp.cumsum(lc, axis=1)
        o_inter = jnp.einsum('bchk,bhkv->bchv', qc * jnp.exp(b), S)
        diff = b[:, :, None] - b[:, None, :]
        decay = jnp.where(causal, jnp.exp(jnp.where(causal, diff, 0.0)), 0.0)
        att = jnp.einsum('bthk,btshk,bshk->bhts', qc, decay, kc)
        o_intra = jnp.einsum('bhts,bshv->bthv', att, vc)
        b_last = b[:, -1:]
        S_new = jnp.exp(b_last[:, 0])[..., None] * S + jnp.einsum('bshk,bshv->bhkv', kc * jnp.exp(b_last - b), vc)
        return S_new, o_inter + o_intra

    S, o = lax.scan(step, S0, (to_chunks(q), to_chunks(k), to_chunks(logf), to_chunks(v)))
    o = o.transpose(1, 0, 2, 3, 4).reshape(B, n * C, H, V)[:, :T]
    return o, S


def hgrn2_mixer(x, S0, lb, w_in, g_onorm, w_out):
    B, T, D = x.shape
    f32 = jnp.float32
    q, fz, i, g = jnp.split(x @ w_in, 4, axis=-1)
    lb32 = lb.astype(f32)
    logf = jnp.logaddexp(jnp.log(lb32), jnp.log1p(-lb32) + jax.nn.log_sigmoid(fz.astype(f32)))
    k = -jnp.expm1(logf)
    hs = lambda a: a.reshape(B, T, REC_HEADS, REC_HEAD_DIM)
    o, S = gla_chunked(hs(jax.nn.silu(q.astype(f32))), hs(k), hs(logf), hs(i.astype(f32)), S0.astype(f32))
    o = rmsnorm(o, g_onorm) * jax.nn.silu(hs(g.astype(f32)))
    y = o.reshape(B, T, D).astype(x.dtype) @ w_out
    return y, S.astype(S0.dtype)


def mem_kv(mem, g_mem, w_kv):
    B, N, _ = mem.shape
    k, v = jnp.split(rmsnorm(mem, g_mem) @ w_kv, 2, axis=-1)
    return k.reshape(B, N, X_HEADS, X_HEAD_DIM), v.reshape(B, N, X_HEADS, X_HEAD_DIM)


def cross_attn(x, mk, mv, w_q, w_o):
    B, T, D = x.shape
    q = (x @ w_q).reshape(B, T, X_HEADS, X_HEAD_DIM)
    s = jnp.einsum('bthd,bnhd->bhtn', q, mk.astype(q.dtype)).astype(jnp.float32) / math.sqrt(X_HEAD_DIM)
    p = jax.nn.softmax(s, axis=-1).astype(x.dtype)
    o = jnp.einsum('bhtn,bnhd->bthd', p, mv.astype(x.dtype)).reshape(B, T, D)
    return o @ w_o


def setup_inputs(seed: int = 0) -> dict:
    key = jax.random.key(seed)
    ks = iter(jax.random.split(key, 40))
    nrm = lambda shape, scale: jax.random.normal(next(ks), shape, jnp.float32) * scale
    gain = lambda shape: 1.0 + 0.05 * jax.random.normal(next(ks), shape, jnp.float32)
    D = D_MODEL
    return {
        "x_prompt": nrm((BATCH, SEQ, D), 1.0),
        "x_sample": nrm((DEC_BATCH, DEC_SEQ, D), 1.0),
        "mem_prompt": nrm((BATCH, N_MEM, D), 1.0),
        "state_conv": nrm((N_CONV_LAYERS, DEC_BATCH, CONV_WIDTH - 1, D), 1.0),
        "state_rec": nrm((N_REC_LAYERS, DEC_BATCH, REC_HEADS, REC_HEAD_DIM, REC_HEAD_DIM), 0.3),
        "cache_mem_k": nrm((DEPTH, DEC_BATCH, N_MEM, X_HEADS, X_HEAD_DIM), 1.0),
        "cache_mem_v": nrm((DEPTH, DEC_BATCH, N_MEM, X_HEADS, X_HEAD_DIM), 1.0),
        "norm_ffn1": gain((DEPTH, D)),
        "w_ffn1_gate": nrm((DEPTH, D, D_FF), D ** -0.5),
        "w_ffn1_up": nrm((DEPTH, D, D_FF), D ** -0.5),
        "w_ffn1_down": nrm((DEPTH, D_FF, D), D_FF ** -0.5),
        "norm_mix": gain((DEPTH, D)),
        "w_conv_in": nrm((N_CONV_LAYERS, D, 3 * D), D ** -0.5),
        "w_conv": nrm((N_CONV_LAYERS, CONV_WIDTH, D), CONV_WIDTH ** -0.5),
        "w_conv_out": nrm((N_CONV_LAYERS, D, D), D ** -0.5),
        "lb_raw": nrm((DEPTH, D), 0.1),
        "w_rec_in": nrm((N_REC_LAYERS, D, 4 * D), D ** -0.5),
        "g_rec_onorm": gain((N_REC_LAYERS, REC_HEAD_DIM)),
        "w_rec_out": nrm((N_REC_LAYERS, D, D), D ** -0.5),
        "norm_xattn": gain((DEPTH, D)),
        "norm_mem": gain((DEPTH, D)),
        "w_xq": nrm((DEPTH, D, D), D ** -0.5),
        "w_xkv": nrm((DEPTH, D, 2 * D), D ** -0.5),
        "w_xo": nrm((DEPTH, D, D), D ** -0.5),
        "norm_ffn2": gain((DEPTH, D)),
        "w_ffn2_gate": nrm((DEPTH, D, D_FF), D ** -0.5),
        "w_ffn2_up": nrm((DEPTH, D, D_FF), D ** -0.5),
        "w_ffn2_down": nrm((DEPTH, D_FF, D), D_FF ** -0.5),
        "norm_final": gain((D,)),
    }


def reference(x_prompt, x_sample, mem_prompt, state_conv, state_rec, cache_mem_k, cache_mem_v,
              norm_ffn1, w_ffn1_gate, w_ffn1_up, w_ffn1_down, norm_mix,
              w_conv_in, w_conv, w_conv_out, lb_raw, w_rec_in, g_rec_onorm, w_rec_out,
              norm_xattn, norm_mem, w_xq, w_xkv, w_xo,
              norm_ffn2, w_ffn2_gate, w_ffn2_up, w_ffn2_down, norm_final):
    lower_bounds = hgrn2_lower_bounds(lb_raw)

    def trunk(x, conv_state, rec_state, mem_k, mem_v):
        new_conv, new_rec = [], []
        for i in range(DEPTH):
            h = rmsnorm(x, norm_ffn1[i])
            x = x + 0.5 * swiglu(h, w_ffn1_gate[i], w_ffn1_up[i], w_ffn1_down[i])
            h = rmsnorm(x, norm_mix[i])
            j = i // N_MIXERS
            if i % N_MIXERS == 0:
                y, buf = short_conv_mixer(h, conv_state[j], w_conv_in[j], w_conv[j], w_conv_out[j])
                new_conv.append(buf)
            else:
                y, S = hgrn2_mixer(h, rec_state[j], lower_bounds[i], w_rec_in[j], g_rec_onorm[j], w_rec_out[j])
                new_rec.append(S)
            x = x + y
            h = rmsnorm(x, norm_xattn[i])
            x = x + cross_attn(h, mem_k[i], mem_v[i], w_xq[i], w_xo[i])
            h = rmsnorm(x, norm_ffn2[i])
            x = x + 0.5 * swiglu(h, w_ffn2_gate[i], w_ffn2_up[i], w_ffn2_down[i])
        return rmsnorm(x, norm_final), jnp.stack(new_conv), jnp.stack(new_rec)

    B = x_prompt.shape[0]
    kv_p = [mem_kv(mem_prompt, norm_mem[i], w_xkv[i]) for i in range(DEPTH)]
    mem_k_p = jnp.stack([kv[0] for kv in kv_p])
    mem_v_p = jnp.stack([kv[1] for kv in kv_p])
    conv0 = jnp.zeros((N_CONV_LAYERS, B, CONV_WIDTH - 1, D_MODEL), x_prompt.dtype)
    rec0 = jnp.zeros((N_REC_LAYERS, B, REC_HEADS, REC_HEAD_DIM, REC_HEAD_DIM), x_prompt.dtype)
    y_prompt, conv_p, rec_p = trunk(x_prompt, conv0, rec0, mem_k_p, mem_v_p)

    y_sample, conv_s, rec_s = trunk(x_sample, state_conv, state_rec, cache_mem_k, cache_mem_v)

    return (y_prompt, y_sample, mem_k_p, mem_v_p, conv_p, rec_p, conv_s, rec_s)
```

```python
import contextlib
import numpy as np
import concourse.bass as bass
import concourse.mybir as mybir
from concourse.bass_utils import run_bass_kernel_spmd

F32 = mybir.dt.float32
BF16 = mybir.dt.bfloat16
AF = mybir.ActivationFunctionType
ALU = mybir.AluOpType
AX = mybir.AxisListType
COMPUTE = ("pe", "act", "dve", "pool")
D = 1024
DFF = 2816
TP = 1024
NS = 16
EPS = 1e-6
NSLOT = 3


class Op:
    __slots__ = ("idx", "eng", "fn", "deps", "raw", "is_dma", "semkey", "signal", "seq", "wk")


class Prog:
    def __init__(self, nc):
        self.nc = nc
        self.ops = []
        self.last_writer = {}
        self.readers = {}
        self.span = {}
        self.blocks = {}
        self.h = {"pe": nc.tensor, "act": nc.scalar, "dve": nc.vector, "pool": nc.gpsimd, "sp": nc.sync}

    def op(self, eng, fn, reads=(), writes=(), dma=False, semkey=None):
        o = Op()
        o.idx = len(self.ops)
        o.eng, o.fn, o.is_dma, o.semkey, o.signal, o.seq = eng, fn, dma, semkey, False, None
        deps = set()
        lw, rd = self.last_writer, self.readers
        xw = [k for k in reads if k == "psb" or (isinstance(k, tuple) and k[0] == "ps")]
        if xw:
            writes = list(writes) + [k for k in xw if k not in writes]
        raw = set()
        for k in reads:
            w = lw.get(k)
            if w is not None:
                deps.add(w)
                raw.add(w)
        o.raw = raw
        o.wk = list(writes)
        for k in writes:
            w = lw.get(k)
            if w is not None:
                deps.add(w)
            deps.update(rd.get(k, ()))
            sp = self.span.get(k)
            if sp is not None:
                a, b = sp
                seen = set()
                for blk in range(a // 256, (b - 1) // 256 + 1):
                    st_ = self.blocks.get(blk)
                    if st_ is None:
                        st_ = self.blocks[blk] = set()
                    for k2 in st_:
                        if k2 == k or k2 in seen:
                            continue
                        seen.add(k2)
                        a2, b2 = self.span[k2]
                        if a2 < b and a < b2:
                            w2 = lw.get(k2)
                            if w2 is not None:
                                deps.add(w2)
                            deps.update(rd.get(k2, ()))
                    if blk * 256 >= a and (blk + 1) * 256 <= b:
                        st_.clear()
                    st_.add(k)
        deps.discard(o.idx)
        o.deps = deps
        for k in reads:
            rd.setdefault(k, []).append(o.idx)
        for k in writes:
            lw[k] = o.idx
            rd[k] = []
        self.ops.append(o)
        return o

    def emit(self, final_wait_ops=()):
        nc, ops = self.nc, self.ops
        needed = []
        for o in ops:
            best = {}
            for d in o.deps:
                p = ops[d]
                if p.is_dma or o.is_dma or p.eng != o.eng or p.eng != "pe":
                    sk = ("dma", p.semkey) if p.is_dma else ("eng", p.eng)
                    if d > best.get(sk, -1):
                        best[sk] = d
            nd = list(best.values())
            needed.append(nd)
            for d in nd:
                ops[d].signal = True
        for d in final_wait_ops:
            d.signal = True
        eng_cnt = {e: 0 for e in COMPUTE}
        dma_cnt = {}
        for o in ops:
            if o.is_dma:
                dma_cnt[o.semkey] = dma_cnt.get(o.semkey, 0) + 16
                o.seq = dma_cnt[o.semkey]
            elif o.signal:
                eng_cnt[o.eng] += 1
                o.seq = eng_cnt[o.eng]
        for o in ops:
            if o.is_dma and isinstance(o.semkey, str) and o.semkey.startswith("G:"):
                o.seq = dma_cnt[o.semkey]
        sems = {}
        stack = contextlib.ExitStack()
        for e in COMPUTE:
            sems[("eng", e)] = stack.enter_context(nc.semaphore("s_" + e))
        for i, k in enumerate(dma_cnt):
            sems[("dma", k)] = stack.enter_context(nc.semaphore("d%d" % i))
        waited = {}
        for o in ops:
            h = self.h[o.eng]
            w = {}
            for d in needed[o.idx]:
                p = ops[d]
                sk = ("dma", p.semkey) if p.is_dma else ("eng", p.eng)
                if p.seq > w.get(sk, 0):
                    w[sk] = p.seq
            for sk, v in w.items():
                if waited.get((o.eng, sk), 0) >= v:
                    continue
                waited[(o.eng, sk)] = v
                h.wait_ge(sems[sk], v)
            ins = o.fn(h)
            if o.is_dma:
                ins.then_inc(sems[("dma", o.semkey)], 16)
            elif o.signal:
                ins.then_inc(sems[("eng", o.eng)], 1)
        w = {}
        for p in final_wait_ops:
            sk = ("dma", p.semkey) if p.is_dma else ("eng", p.eng)
            if p.seq > w.get(sk, 0):
                w[sk] = p.seq
        for sk, v in w.items():
            nc.sync.wait_ge(sems[sk], v)
        return stack


class Buf:
    def __init__(self, K, name, C, T, dtype, subs=None, arena_off=None, tensor=None):
        self.K, self.name, self.C, self.T = K, name, C, T
        self.es = 2 if dtype == BF16 else 4
        self.subs = subs if subs is not None else [(0, T)]
        self.off = arena_off
        if tensor is not None:
            self.ap = tensor[:]
        else:
            nb = C * T * self.es
            a = arena_off // 4
            b = (arena_off + nb + 3) // 4
            v = K.AR[:, a:b]
            if dtype == BF16:
                v = v.bitcast(BF16)
            v = v[:, 0:C * T]
            self.ap = v.rearrange("p (c t) -> p c t", t=T)

    def k(self, c, si):
        key = (self.name, c, si, self.T, self.off)
        if self.off is not None:
            o, n = self.subs[si]
            s0 = self.off + (c * self.T + o) * self.es
            self.K.P.span[key] = (s0, s0 + n * self.es)
        return key

    def ks(self, cs, t0=0, n=None):
        if n is None:
            n = self.T - t0
        if isinstance(cs, int):
            cs = [cs]
        out = []
        for si, (o, m) in enumerate(self.subs):
            if o < t0 + n and t0 < o + m:
                for c in cs:
                    out.append(self.k(c, si))
        return out

    def all(self):
        return self.ks(range(self.C))


class KB:
    pass


class Stop(Exception):
    pass


import os
KSTOP = int(os.environ.get("KSTOP", "-1"))


def build_program():
    nc = bass.Bass("TRN2", target_bir_lowering=False)
    K = KB()
    P = Prog(nc)
    K.P = P
    st = contextlib.ExitStack()
    din = lambda n, s: nc.dram_tensor(n, s, F32, kind="ExternalInput").ap()
    dout = lambda n, s: nc.dram_tensor(n, s, F32, kind="ExternalOutput").ap()
    xp = din("xp", [2048, D]); xs = din("xs", [NS, D]); mem = din("mem", [256, D])
    sconv = din("sconv", [NS, 2, D]); srec = din("srec", [NS, 8, 128, 128])
    ck = din("ck", [2, NS, 256, D]); cv = din("cv", [2, NS, 256, D])
    n_ffn1 = din("norm_ffn1", [2, D]); wg1 = din("w_ffn1_gate", [2, D, DFF]); wu1 = din("w_ffn1_up", [2, D, DFF]); wd1 = din("w_ffn1_down", [2, DFF, D])
    n_mix = din("norm_mix", [2, D]); wci = din("w_conv_in", [1, D, 3 * D]); wcv = din("w_conv", [1, 3, D]); wco = din("w_conv_out", [1, D, D])
    lbr = din("lb_raw", [2, D]); wri = din("w_rec_in", [1, D, 4 * D]); gro = din("g_rec_onorm", [1, 128]); wro = din("w_rec_out", [1, D, D])
    n_xa = din("norm_xattn", [2, D]); n_mem = din("norm_mem", [2, D]); wxq = din("w_xq", [2, D, D]); wxkv = din("w_xkv", [2, D, 2 * D]); wxo = din("w_xo", [2, D, D])
    n_ffn2 = din("norm_ffn2", [2, D]); wg2 = din("w_ffn2_gate", [2, D, DFF]); wu2 = din("w_ffn2_up", [2, D, DFF]); wd2 = din("w_ffn2_down", [2, DFF, D])
    n_fin = din("norm_final", [D])
    yp = dout("yp", [2048, D]); ys = dout("ys", [NS, D]); mk = dout("mk", [2, 256, D]); mv = dout("mv", [2, 256, D])
    cpo = dout("cp", [2, D]); rpo = dout("rp", [8, 128, 128]); cso = dout("cs", [NS, 2, D]); rso = dout("rs", [NS, 8, 128, 128])

    sbt = lambda n, s, dt=F32: st.enter_context(nc.sbuf_tensor(n, s, dt))
    NTB = TP + NS
    SUBB = [(0, 512), (512, 512), (TP, NS)]
    Xb = Buf(K, "X", 8, NTB, F32, SUBB, tensor=sbt("X", [128, 8, NTB]))
    Hb = Buf(K, "H", 8, NTB, BF16, SUBB, tensor=sbt("H", [128, 8, NTB], BF16))
    WS = [sbt("ws%d" % i, [128, 4096], BF16) for i in range(NSLOT)]
    KTb = sbt("KTb", [128, 2, 8, 256], BF16)
    Vbb = sbt("Vbb", [128, 2, 2, D], BF16)
    VC = sbt("VC", [128, 8, 16])
    LBt = sbt("LB", [128, 8]); OML = sbt("OML", [128, 8]); NOML = sbt("NOML", [128, 8]); DLT = sbt("DLT", [128, 8])
    HOML = sbt("HOML", [128, 8]); NHOML = sbt("NHOML", [128, 8]); LBH = sbt("LBH", [128, 8])
    gon = sbt("gon", [128, 1]); epsc = sbt("epsc", [128, 1])
    identf = sbt("identf", [128, 128]); identb = sbt("identb", [128, 128], BF16)
    onesD = sbt("onesD", [128, 128], BF16); ones1 = sbt("ones1", [128, 128], BF16); onesH = sbt("onesH", [128, 128], BF16)
    cm = sbt("cm", [128, 64]); SM = sbt("SM", [128, NTB])
    selb = sbt("selb", [16, 16, 128], BF16)
    Sf = sbt("Sf", [128, 8, 128]); Sbf = sbt("Sbf", [128, 8, 128], BF16)
    convst = sbt("convst", [128, 8, 2])
    SQ = sbt("SQ", [128, 8, 512], BF16)
    RSa = sbt("RSa", [128, 512]); RSb = sbt("RSb", [128, 512])
    SGs = [sbt("SG%d" % i, [128, 512]) for i in range(2)]
    ST0 = sbt("ST0", [128, 8, NS]); ST1 = sbt("ST1", [128, 8, NS]); NV = sbt("NV", [128, 8, NS])
    QTK = sbt("QTK", [NS, D], BF16); IST = sbt("IST", [NS, D], BF16)
    KKs = sbt("KKs", [128, 8, NS]); Fs = sbt("Fs", [128, 8, NS]); QSs = sbt("QSs", [128, 8, NS])
    ESs = sbt("ESs", [128, NS, 8], BF16); SC = sbt("SC", [128, 8]); RDs = sbt("RDs", [128, NS, 4]); SCs = sbt("SCs", [128, NS, 8])
    ARENA = 74 * 1024
    K.AR = sbt("AR", [128, ARENA // 4])
    PS = [st.enter_context(nc.psum_tensor("ps%d" % i, [128, 512], F32)) for i in range(7)]
    PSB = st.enter_context(nc.psum_tensor("psb", [128, 1024], BF16))
    pk = lambda b: ("ps", b)
    rr = [0]

    def nb():
        b = rr[0]
        rr[0] = (b + 1) % 7
        return b

    flip = [0]

    def cp_eng():
        flip[0] ^= 1
        return "act" if flip[0] else "dve"

    def copy(eng, out, in_, reads, writes, scale=None):
        if eng == "act":
            sc_ = 1.0 if scale is None else scale
            P.op("act", lambda e: e.activation(out=out, in_=in_, func=AF.Copy, scale=sc_), reads, writes)
        else:
            if scale is None:
                P.op(eng, lambda e: e.tensor_copy(out=out, in_=in_), reads, writes)
            else:
                P.op(eng, lambda e: e.tensor_scalar(out=out, in0=in_, scalar1=scale, scalar2=None, op0=ALU.mult), reads, writes)

    def mm(out, lhsT, rhs, start, stop, reads, writes):
        P.op("pe", lambda e: e.matmul(out, lhsT=lhsT, rhs=rhs, start=start, stop=stop), reads, writes)

    def tr(out, in_, ident, reads, writes):
        P.op("pe", lambda e: e.transpose(out=out, in_=in_, identity=ident), reads, writes)

    def act(out, in_, func, reads, writes, bias=None, scale=1.0):
        if bias is None:
            P.op("act", lambda e: e.activation(out=out, in_=in_, func=func, scale=scale), reads, writes)
        else:
            P.op("act", lambda e: e.activation(out=out, in_=in_, func=func, bias=bias, scale=scale), reads, writes)

    def tt(out, in0, in1, op, reads, writes, eng="dve"):
        P.op(eng, lambda e: e.tensor_tensor(out=out, in0=in0, in1=in1, op=op), reads, writes)

    def stt(out, in0, scalar, in1, op0, op1, reads, writes):
        P.op("dve", lambda e: e.scalar_tensor_tensor(out=out, in0=in0, scalar=scalar, in1=in1, op0=op0, op1=op1), reads, writes)

    def dma(eng, out, in_, reads, writes, semkey):
        return P.op(eng, lambda e: e.dma_start(out=out, in_=in_), reads, writes, dma=True, semkey=semkey)

    outs = []
    MARKS = []
    K.MARKS = MARKS

    wv = lambda w: w.rearrange("(c p) n -> p c n", p=128)
    sched = []

    def U(name, parts):
        sched.append((name, parts))

    def sched_ffn(l, wg, wu, wd, tag):
        for g in range(11):
            U((tag, "gu", l, g), [(wv(wg[l])[:, :, g * 256:(g + 1) * 256], 8, 256), (wv(wu[l])[:, :, g * 256:(g + 1) * 256], 8, 256)])
        for c in range(8):
            U((tag, "d", l, c), [(wv(wd[l])[:, :, c * 128:(c + 1) * 128], 22, 128)])

    def sched_xattn(l):
        for hf in range(2):
            U(("xq", l, hf), [(wv(wxq[l])[:, :, hf * 512:(hf + 1) * 512], 8, 512)])
        for hf in range(2):
            U(("xo", l, hf), [(wv(wxo[l])[:, :, hf * 512:(hf + 1) * 512], 8, 512)])

    for l in range(2):
        for hf in range(2):
            U(("xk", l, hf), [(wv(wxkv[l])[:, :, hf * 512:(hf + 1) * 512], 8, 512)])
        for hf in range(2):
            U(("xv", l, hf), [(wv(wxkv[l])[:, :, D + hf * 512:D + (hf + 1) * 512], 8, 512)])
    for pas in range(2):
        sched_ffn(0, wg1, wu1, wd1, "f1")
        for c in range(8):
            U(("ci", c), [(wv(wci[0])[:, :, pt * D + c * 128: pt * D + (c + 1) * 128], 8, 128) for pt in range(3)])
        for hf in range(2):
            U(("co", hf), [(wv(wco[0])[:, :, hf * 512:(hf + 1) * 512], 8, 512)])
        sched_xattn(0)
        sched_ffn(0, wg2, wu2, wd2, "f2")
        sched_ffn(1, wg1, wu1, wd1, "f1")
        for g in range(2):
            U(("rv", g), [(wv(wri[0])[:, :, 2 * D + g * 512: 2 * D + (g + 1) * 512], 8, 512)])
            for hl in range(4):
                hh = g * 4 + hl
                U(("rq", hh), [(wv(wri[0])[:, :, pt * D + hh * 128: pt * D + (hh + 1) * 128], 8, 128) for pt in (0, 1, 3)])
            U(("ro", g), [(wro[0][g * 512:(g + 1) * 512, :].rearrange("(c p) n -> p c n", p=128), 4, D)])
        sched_xattn(1)
        sched_ffn(1, wg2, wu2, wd2, "f2")
    wpos = [0, 0]

    def issue_upto(i):
        while wpos[1] <= i and wpos[1] < len(sched):
            j = wpos[1]
            slot = j % NSLOT
            off = 0
            first = None
            for pi, (src, kc, n) in enumerate(sched[j][1]):
                dst = WS[slot][:, off:off + kc * n].rearrange("p (c n) -> p c n", n=n)
                if pi == 0:
                    first = dma("pool", dst, src, [], [("ws", slot, q) for q in range(3)], "w%d_%d" % (slot, pi))
                else:
                    o = dma("pool", dst, src, [], [], "w%d_%d" % (slot, pi))
                    o.deps = set(first.deps)
                    P.last_writer[("ws", slot, pi)] = o.idx
                    P.readers[("ws", slot, pi)] = []
                off += kc * n
            wpos[1] += 1

    def wget(name):
        i = wpos[0]
        assert sched[i][0] == name, (sched[i][0], name)
        issue_upto(i + NSLOT - 2)
        wpos[0] += 1
        slot = i % NSLOT
        res = []
        off = 0
        for pi, (src, kc, n) in enumerate(sched[i][1]):
            res.append((WS[slot][:, off:off + kc * n].rearrange("p (c n) -> p c n", n=n), ("ws", slot, pi)))
            off += kc * n
        return res

    P.op("pool", lambda e: e.memset(identf[:], 1.0), [], ["identf"])
    P.op("pool", lambda e: e.affine_select(out=identf[:], in_=identf[:], pattern=[[-1, 128]], compare_op=ALU.is_equal, fill=0.0,
                                           base=0, channel_multiplier=1), ["identf"], ["identf"])
    copy("dve", identb[:], identf[:], ["identf"], ["identb"])
    P.op("dve", lambda e: e.memset(onesD[:], 1.0 / D), [], ["onesD"])
    P.op("dve", lambda e: e.memset(ones1[:], 1.0), [], ["ones1"])
    P.op("dve", lambda e: e.memset(onesH[:], 1.0 / 128), [], ["onesH"])
    P.op("dve", lambda e: e.memset(epsc[:], EPS), [], ["epsc"])
    P.op("dve", lambda e: e.memset(Sf[:], 0.0), [], ["Sf%d" % h for h in range(8)])
    P.op("dve", lambda e: e.memset(Sbf[:], 0.0), [], ["Sbh%d" % h for h in range(8)])
    P.op("dve", lambda e: e.memset(convst[:], 0.0), [], ["convst%d" % c for c in range(8)])
    P.op("dve", lambda e: e.memset(SM[:], 1.0), [], ["SM"])
    P.op("dve", lambda e: e.memset(SM[:, 0:TP].rearrange("p (c t) -> p c t", t=64)[:, :, 0:1], 0.0), ["SM"], ["SM"])
    P.op("dve", lambda e: e.memset(SM[:, TP:NTB], 0.0), ["SM"], ["SM"])
    P.op("pool", lambda e: e.memset(cm[:], 1.0), [], ["cm"])
    for hf in range(2):
        P.op("pool", lambda e, hf=hf: e.affine_select(out=cm[hf * 64:(hf + 1) * 64, :], in_=cm[hf * 64:(hf + 1) * 64, :], pattern=[[1, 64]],
                                                      compare_op=ALU.is_ge, fill=0.0, base=0, channel_multiplier=-1), ["cm"], ["cm"])
    P.op("pool", lambda e: e.memset(selb[:], 1.0), [], ["selb"])
    P.op("pool", lambda e: e.affine_select(out=selb[:], in_=selb[:], pattern=[[-1, 16], [0, 128]], compare_op=ALU.is_equal, fill=0.0,
                                           base=0, channel_multiplier=1), ["selb"], ["selb"])
    rows = [n_ffn1[0:1, :], n_ffn1[1:2, :], n_mix[0:1, :], n_mix[1:2, :], n_xa[0:1, :], n_xa[1:2, :], n_mem[0:1, :], n_mem[1:2, :],
            n_ffn2[0:1, :], n_ffn2[1:2, :], n_fin.rearrange("(o d) -> o d", o=1), wcv[0, 0:1, :], wcv[0, 1:2, :], wcv[0, 2:3, :],
            lbr[0:1, :], lbr[1:2, :]]
    VRb = Buf(K, "VR", 1, D, F32, arena_off=40960)
    VR = VRb.ap[:, 0, :]
    for r, src in enumerate(rows):
        dma("sp", VR[r:r + 1, :], src, [], [("VRr", r)], "G:c")
    b = nb()
    for c in range(8):
        tr(PS[b][:, c * 16:(c + 1) * 16], VR[0:16, c * 128:(c + 1) * 128], identf[0:16, 0:16], ["identf"] + [("VRr", r) for r in range(16)], [pk(b)])
    copy("dve", VC[:], PS[b][:, 0:128].rearrange("p (c r) -> p c r", r=16), [pk(b)], ["VC"])
    P.op("sp", lambda e: e.dma_start(out=gon[:], in_=gro.rearrange("o p -> p o")), [], ["gon"], dma=True, semkey="G:c")
    tt(DLT[:], VC[:, :, 15], VC[:, :, 14], ALU.subtract, ["VC"], ["DLT"])
    act(LBt[:], DLT[:], AF.Sigmoid, ["DLT"], ["LB"])
    act(OML[:], DLT[:], AF.Sigmoid, ["DLT"], ["OML"], scale=-1.0)
    P.op("dve", lambda e: e.tensor_scalar(out=NOML[:], in0=OML[:], scalar1=-1.0, scalar2=None, op0=ALU.mult), ["OML"], ["NOML"])
    P.op("dve", lambda e: e.tensor_scalar(out=HOML[:], in0=OML[:], scalar1=0.5, scalar2=None, op0=ALU.mult), ["OML"], ["HOML"])
    P.op("dve", lambda e: e.tensor_scalar(out=NHOML[:], in0=OML[:], scalar1=-0.5, scalar2=None, op0=ALU.mult), ["OML"], ["NHOML"])
    tt(LBH[:], LBt[:], HOML[:], ALU.add, ["LB", "HOML"], ["LBH"])
    gc = lambda r, c: VC[:, c, r:r + 1]

    if KSTOP == -2:
        outs.append(dma("sp", ys, SM[0:16, 0:1024], ["VC", "gon", "LB", "OML", "NOML", "selb", "cm", "SM", "identb"], [], "o_dbg"))
        P.emit(final_wait_ops=outs)
        return nc
    norm_done = set()

    def norm_sub(src, grow, dst, subs, si, tag):
        if (tag, si) in norm_done:
            return
        norm_done.add((tag, si))
        off, n = subs[si]
        act(SQ[:, :, 0:n], src.ap[:, :, off:off + n], AF.Square, src.ks(range(8), off, n), ["SQ"])
        b = nb()
        for c in range(8):
            mm(PS[b][:, 0:n], onesD[:], SQ[:, c, 0:n], c == 0, c == 7, ["onesD", "SQ"], [pk(b)])
        act(RSa[:, 0:n], PS[b][:, 0:n], AF.Ln, [pk(b), "epsc"], ["RSa"], bias=epsc[:])
        act(RSb[:, 0:n], RSa[:, 0:n], AF.Exp, ["RSa"], ["RSb"], scale=-0.5)
        for c in range(8):
            stt(dst.ap[:, c, off:off + n], src.ap[:, c, off:off + n], gc(grow, c), RSb[:, 0:n], ALU.mult, ALU.mult,
                src.ks(c, off, n) + ["VC", "RSb"], dst.ks(c, off, n))

    ntag = [0]

    def rmsnorm(src, grow, dst, subs, tag=None):
        if tag is None:
            ntag[0] += 1
            tag = ("anon", ntag[0])
        for si in range(len(subs)):
            norm_sub(src, grow, dst, subs, si, tag)

    def proj_add(wlist, nk, src, subs, scale, after_sub=None):
        per = 8 // len(wlist)
        for si, (off, n) in enumerate(subs):
            for c in range(8):
                wt, wkey = wlist[c // per]
                c0 = (c % per) * 128
                b = nb()
                for k in range(nk):
                    mm(PS[b][:, 0:n], wt[:, k, c0:c0 + 128], src.ap[:, k, off:off + n], k == 0, k == nk - 1,
                       [wkey] + src.ks(k, off, n), [pk(b)])
                stt(Xb.ap[:, c, off:off + n], PS[b][:, 0:n], scale, Xb.ap[:, c, off:off + n], ALU.mult, ALU.add,
                    [pk(b)] + Xb.ks(c, off, n), Xb.ks(c, off, n))
            if after_sub is not None:
                after_sub(si)

    def ffn(l, which, subs, ntag_, after_sub=None):
        tag = "f1" if which == 1 else "f2"
        rmsnorm(Xb, (l if which == 1 else 8 + l), Hb, subs, ntag_)
        NT = subs[-1][0] + subs[-1][1]
        Ub = Buf(K, "U", 22, NT, BF16, subs, arena_off=0)
        r = 0
        for g in range(11):
            (gt, gk), (ut, uk) = wget((tag, "gu", l, g))
            for jj in range(2):
                j = g * 2 + jj
                for si, (off, n) in enumerate(subs):
                    bg, bu = nb(), nb()
                    for k in range(8):
                        mm(PS[bg][:, 0:n], gt[:, k, jj * 128:(jj + 1) * 128], Hb.ap[:, k, off:off + n], k == 0, k == 7, [gk] + Hb.ks(k, off, n), [pk(bg)])
                    for k in range(8):
                        mm(PS[bu][:, 0:n], ut[:, k, jj * 128:(jj + 1) * 128], Hb.ap[:, k, off:off + n], k == 0, k == 7, [uk] + Hb.ks(k, off, n), [pk(bu)])
                    sg = SGs[r % 2]
                    sgk = "SG%d" % (r % 2)
                    r += 1
                    act(sg[:, 0:n], PS[bg][:, 0:n], AF.Silu, [pk(bg)], [sgk])
                    tt(Ub.ap[:, j, off:off + n], sg[:, 0:n], PS[bu][:, 0:n], ALU.mult, [sgk, pk(bu)], Ub.ks(j, off, n))
        for c2 in range(4):
            dl = [wget((tag, "d", l, c2 * 2))[0], wget((tag, "d", l, c2 * 2 + 1))[0]]
            if c2 < 3:
                order = [(si, cc) for cc in range(2) for si in range(len(subs))]
            else:
                order = [(si, cc) for si in range(len(subs)) for cc in range(2)]
            for si, cc in order:
                off, n = subs[si]
                c = c2 * 2 + cc
                dt_, dk = dl[cc]
                b = nb()
                for j in range(22):
                    mm(PS[b][:, 0:n], dt_[:, j, 0:128], Ub.ap[:, j, off:off + n], j == 0, j == 21, [dk] + Ub.ks(j, off, n), [pk(b)])
                stt(Xb.ap[:, c, off:off + n], PS[b][:, 0:n], 0.5, Xb.ap[:, c, off:off + n], ALU.mult, ALU.add,
                    [pk(b)] + Xb.ks(c, off, n), Xb.ks(c, off, n))
                if c2 == 3 and cc == 1 and after_sub is not None:
                    after_sub(si)

    def to_tok(src_fn, ncols, dst_tile, dst_key, reads):
        for hf in range(2):
            b = nb()
            for cc in range(4):
                c = hf * 4 + cc
                tr(PS[b][0:ncols, cc * 128:(cc + 1) * 128], src_fn(c), identf[:], reads + ["identf"], [pk(b)])
            copy(cp_eng(), dst_tile[0:ncols, hf * 512:(hf + 1) * 512], PS[b][0:ncols, :], [pk(b)], [dst_key])

    XTb = [Buf(K, "XT%d" % i, 1, D, F32, arena_off=45056 + i * 4096) for i in range(2)]
    xpre = set()

    def prefetch_x(pas):
        for tb in range(2):
            dma("sp", XTb[tb].ap[:, 0, :], xp[pas * TP + tb * 128: pas * TP + (tb + 1) * 128, :], [], XTb[tb].all(), "xt%d" % tb)
            xpre.add((pas, tb))

    MT = Buf(K, "MT", 2, D, F32, arena_off=0)
    MX = Buf(K, "MX", 8, 256, F32, arena_off=8192)
    MH = Buf(K, "MH", 8, 256, BF16, arena_off=16384)
    OK_ = Buf(K, "OK", 2, D, F32, arena_off=20480)
    OV_ = Buf(K, "OV", 2, D, F32, arena_off=28672)
    dma("sp", MT.ap, mem.rearrange("(c p) d -> p c d", p=128), [], MT.all(), "G:c")
    prefetch_x(0)
    for nc_ in range(2):
        for hf in range(2):
            b = nb()
            for cc in range(4):
                c = hf * 4 + cc
                tr(PS[b][:, cc * 128:(cc + 1) * 128], MT.ap[:, nc_, c * 128:(c + 1) * 128], identf[:], MT.all() + ["identf"], [pk(b)])
            copy(cp_eng(), MX.ap[:, hf * 4:(hf + 1) * 4, nc_ * 128:(nc_ + 1) * 128], PS[b][:].rearrange("p (c t) -> p c t", t=128),
                 [pk(b)], MX.ks(range(hf * 4, hf * 4 + 4)))
    for l in range(2):
        rmsnorm(MX, 6 + l, MH, [(0, 256)])
        ktl = [wget(("xk", l, 0))[0], wget(("xk", l, 1))[0]]
        for dc in range(8):
            b = nb()
            kt, kk_ = ktl[dc // 4]
            for k in range(8):
                mm(PS[b][:, 0:256], kt[:, k, (dc % 4) * 128:(dc % 4 + 1) * 128], MH.ap[:, k, :], k == 0, k == 7, [kk_] + MH.ks(k), [pk(b)])
            copy(cp_eng(), KTb[:, l, dc, :], PS[b][:, 0:256], [pk(b)], [("KTb", l)])
        for nc_ in range(2):
            for hf in range(2):
                b = nb()
                kt, kk_ = ktl[hf]
                for k in range(8):
                    mm(PS[b][:], MH.ap[:, k, nc_ * 128:(nc_ + 1) * 128], kt[:, k, :], k == 0, k == 7, [kk_] + MH.ks(k), [pk(b)])
                copy(cp_eng(), OK_.ap[:, nc_, hf * 512:(hf + 1) * 512], PS[b][:], [pk(b)], OK_.all())
        outs.append(dma("sp", mk[l].rearrange("(c p) d -> p c d", p=128), OK_.ap, OK_.all(), [], "o_ok"))
        vtl = [wget(("xv", l, 0))[0], wget(("xv", l, 1))[0]]
        for nc_ in range(2):
            for hf in range(2):
                b = nb()
                vt, vk_ = vtl[hf]
                for k in range(8):
                    mm(PS[b][:], MH.ap[:, k, nc_ * 128:(nc_ + 1) * 128], vt[:, k, :], k == 0, k == 7, [vk_] + MH.ks(k), [pk(b)])
                copy("act", OV_.ap[:, nc_, hf * 512:(hf + 1) * 512], PS[b][:], [pk(b)], OV_.all())
                copy("dve", Vbb[:, l, nc_, hf * 512:(hf + 1) * 512], PS[b][:], [pk(b)], [("Vbb", l)])
        outs.append(dma("sp", mv[l].rearrange("(c p) d -> p c d", p=128), OV_.ap, OV_.all(), [], "o_ov"))

    def run_pass(pas):
        hasS = pas == 1
        subs = SUBB if hasS else SUBB[:2]
        psubs = SUBB[:2]
        NT = NTB if hasS else TP
        t0 = pas * TP
        XT = XTb
        for tb in range(8):
            xt = XT[tb % 2]
            if (pas, tb) not in xpre:
                dma("sp", xt.ap[:, 0, :], xp[t0 + tb * 128: t0 + (tb + 1) * 128, :], [], xt.all(), "xt%d" % (tb % 2))
            for hf in range(2):
                b = nb()
                for cc in range(4):
                    c = hf * 4 + cc
                    tr(PS[b][:, cc * 128:(cc + 1) * 128], xt.ap[:, 0, c * 128:(c + 1) * 128], identf[:], xt.all() + ["identf"], [pk(b)])
                copy(cp_eng(), Xb.ap[:, hf * 4:(hf + 1) * 4, tb * 128:(tb + 1) * 128], PS[b][:].rearrange("p (c t) -> p c t", t=128),
                     [pk(b)], Xb.ks(range(hf * 4, hf * 4 + 4), tb * 128, 128))
        if hasS:
            TOKb = Buf(K, "TOK", 1, D, F32, arena_off=8192)
            SCTb = Buf(K, "SCT", 1, 2 * D, F32, arena_off=12288)
            TOK = TOKb.ap[0:NS, 0, :]
            SCT = SCTb.ap[0:NS, 0, :]
            dma("sp", TOK, xs, [], TOKb.all(), "G:s")
            b = nb()
            for c in range(8):
                tr(PS[b][:, c * 16:(c + 1) * 16], TOK[0:16, c * 128:(c + 1) * 128], identf[0:16, 0:16], TOKb.all() + ["identf"], [pk(b)])
            copy("dve", Xb.ap[:, :, TP:NTB], PS[b][:, 0:128].rearrange("p (c r) -> p c r", r=16), [pk(b)], Xb.ks(range(8), TP, NS))
            dma("sp", SCT, sconv.rearrange("s j d -> s (j d)"), [], SCTb.all(), "G:s")
            for j, STj in enumerate((ST0, ST1)):
                b = nb()
                for c in range(8):
                    tr(PS[b][:, c * 16:(c + 1) * 16], SCT[0:16, j * D + c * 128: j * D + (c + 1) * 128], identf[0:16, 0:16], SCTb.all() + ["identf"], [pk(b)])
                copy("dve", STj[:], PS[b][:, 0:128].rearrange("p (c r) -> p c r", r=16), [pk(b)], ["ST%d" % j])

        def chk(stage):
            MARKS.append(("p%d_s%d" % (pas, stage), sum(1 for o in P.ops if o.eng == "pe")))
            if KSTOP != pas * 20 + stage:
                return
            YT = [Buf(K, "YTd%d" % i, 1, D, F32, arena_off=49152 + i * 4096) for i in range(2)]
            for tb in range(8):
                yt = YT[tb % 2]
                to_tok(lambda c, tb=tb: Xb.ap[:, c, tb * 128:(tb + 1) * 128], 128, yt.ap[:, 0, :], yt.k(0, 0), Xb.ks(range(8), tb * 128, 128))
                outs.append(dma("sp", yp[t0 + tb * 128:t0 + (tb + 1) * 128, :], yt.ap[:, 0, :], yt.all(), [], "o_yt%d" % (tb % 2)))
            if hasS:
                yt = YT[0]
                to_tok(lambda c: Xb.ap[:, c, TP:NT], NS, yt.ap[:, 0, :], yt.k(0, 0), Xb.ks(range(8), TP, NS))
                outs.append(dma("sp", ys, yt.ap[0:NS, 0, :], yt.all(), [], "o_yt0"))
            raise Stop()

        chk(1)

        def nxt(grow, name):
            return lambda si: norm_sub(Xb, grow, Hb, subs, si, (pas, name))

        ffn(0, 1, subs, (pas, "f01"), nxt(2, "conv"))
        chk(2)
        rmsnorm(Xb, 2, Hb, subs, (pas, "conv"))
        BC = Buf(K, "BC", 8, NT, BF16, subs, arena_off=0)
        VBs = [Buf(K, "VB%d" % i, 1, NT + 2, F32, arena_off=17408 + i * 14720) for i in range(2)]
        CVs = [Buf(K, "CV%d" % i, 1, NT, F32, arena_off=17408 + i * 14720 + 4224) for i in range(2)]
        BBfs = [Buf(K, "BBf%d" % i, 1, NT, F32, arena_off=17408 + i * 14720 + 2 * 4224) for i in range(2)]
        UUs = [Buf(K, "UU%d" % i, 1, 512, F32, arena_off=17408 + i * 14720 + 3 * 4224) for i in range(2)]
        for c2 in range(4):
            parts2 = [wget(("ci", c2 * 2)), wget(("ci", c2 * 2 + 1))]
            for cc in range(2):
                c = c2 * 2 + cc
                parts = parts2[cc]
                VB, CV, BBf, UU = VBs[cc], CVs[cc], BBfs[cc], UUs[cc]
                copy("act", VB.ap[:, 0, 0:2], convst[:, c, :], ["convst%d" % c], VB.all())
                for si, (off, n) in enumerate(subs):
                    bs = [nb(), nb(), nb()]
                    for pi in range(3):
                        pt, pkey = parts[pi]
                        for k in range(8):
                            mm(PS[bs[pi]][:, 0:n], pt[:, k, 0:128], Hb.ap[:, k, off:off + n], k == 0, k == 7, [pkey] + Hb.ks(k, off, n), [pk(bs[pi])])
                    copy("act", UU.ap[:, 0, 0:n], PS[bs[2]][:, 0:n], [pk(bs[2])], UU.all())
                    tt(VB.ap[:, 0, 2 + off:2 + off + n], PS[bs[1]][:, 0:n], UU.ap[:, 0, 0:n], ALU.mult, [pk(bs[1])] + UU.all(), VB.all())
                    copy("act", BBf.ap[:, 0, off:off + n], PS[bs[0]][:, 0:n], [pk(bs[0])], BBf.all())
                P.op("dve", lambda e, c=c, CV=CV, VB=VB: e.tensor_scalar(out=CV.ap[:, 0, 0:TP], in0=VB.ap[:, 0, 2:2 + TP], scalar1=gc(13, c), scalar2=None, op0=ALU.mult),
                     VB.all() + ["VC"], CV.all())
                stt(CV.ap[:, 0, 0:TP], VB.ap[:, 0, 1:1 + TP], gc(12, c), CV.ap[:, 0, 0:TP], ALU.mult, ALU.add, VB.all() + CV.all() + ["VC"], CV.all())
                stt(CV.ap[:, 0, 0:TP], VB.ap[:, 0, 0:TP], gc(11, c), CV.ap[:, 0, 0:TP], ALU.mult, ALU.add, VB.all() + CV.all() + ["VC"], CV.all())
                if hasS:
                    P.op("dve", lambda e, c=c, CV=CV, VB=VB: e.tensor_scalar(out=CV.ap[:, 0, TP:NT], in0=VB.ap[:, 0, 2 + TP:2 + NT], scalar1=gc(13, c), scalar2=None, op0=ALU.mult),
                         VB.all() + ["VC"], CV.all())
                    stt(CV.ap[:, 0, TP:NT], ST1[:, c, :], gc(12, c), CV.ap[:, 0, TP:NT], ALU.mult, ALU.add, ["ST1", "VC"] + CV.all(), CV.all())
                    stt(CV.ap[:, 0, TP:NT], ST0[:, c, :], gc(11, c), CV.ap[:, 0, TP:NT], ALU.mult, ALU.add, ["ST0", "VC"] + CV.all(), CV.all())
                    copy("act", NV[:, c, :], VB.ap[:, 0, 2 + TP:2 + NT], VB.all(), ["NV"])
                for si, (off, n) in enumerate(subs):
                    tt(BC.ap[:, c, off:off + n], BBf.ap[:, 0, off:off + n], CV.ap[:, 0, off:off + n], ALU.mult, BBf.all() + CV.all(), BC.ks(c, off, n))
                copy("act", convst[:, c, :], VB.ap[:, 0, TP:TP + 2], VB.all(), ["convst%d" % c])
        col = [wget(("co", 0))[0], wget(("co", 1))[0]]
        proj_add(col, 8, BC, subs, 1.0, nxt(4, "xa0"))
        if hasS:
            CPT = Buf(K, "CPT", 1, D, F32, arena_off=0)
            to_tok(lambda c: convst[:, c, :], 2, CPT.ap[:, 0, :], CPT.k(0, 0), ["convst%d" % c for c in range(8)])
            outs.append(dma("sp", cpo, CPT.ap[0:2, 0, :], CPT.all(), [], "o_cpt"))
            NVT = Buf(K, "NVT", 1, D, F32, arena_off=4096)
            to_tok(lambda c: NV[:, c, :], NS, NVT.ap[:, 0, :], NVT.k(0, 0), ["NV"])
            outs.append(dma("sp", cso[:, 1, :], NVT.ap[0:NS, 0, :], NVT.all(), [], "o_nvt"))
            outs.append(dma("sp", cso[:, 0, :], sconv[:, 1, :], [], [], "o_cs0"))
        chk(3)
        xattn(0, subs, psubs, hasS, NT, (pas, "xa0"), nxt(8, "f02"))
        chk(4)
        ffn(0, 2, subs, (pas, "f02"), nxt(1, "f11"))
        chk(5)
        ffn(1, 1, subs, (pas, "f11"), nxt(3, "rec"))
        chk(6)
        rec(subs, psubs, hasS, NT, (pas, "rec"), nxt(5, "xa1"))
        chk(7)
        xattn(1, subs, psubs, hasS, NT, (pas, "xa1"), nxt(9, "f12"))
        chk(8)
        ffn(1, 2, subs, (pas, "f12"))
        chk(9)
        YF = Buf(K, "YF", 8, NT, F32, subs, arena_off=0)
        if pas == 0 and KSTOP == -1:
            prefetch_x(1)
        rmsnorm(Xb, 10, YF, subs)
        YT = [Buf(K, "YT%d" % i, 1, D, F32, arena_off=36864 + i * 4096) for i in range(2)]
        for tb in range(8):
            yt = YT[tb % 2]
            to_tok(lambda c, tb=tb: YF.ap[:, c, tb * 128:(tb + 1) * 128], 128, yt.ap[:, 0, :], yt.k(0, 0), YF.ks(range(8), tb * 128, 128))
            outs.append(dma("sp", yp[t0 + tb * 128:t0 + (tb + 1) * 128, :], yt.ap[:, 0, :], yt.all(), [], "o_yt%d" % (tb % 2)))
        if hasS:
            yt = YT[0]
            to_tok(lambda c: YF.ap[:, c, TP:NT], NS, yt.ap[:, 0, :], yt.k(0, 0), YF.ks(range(8), TP, NS))
            outs.append(dma("sp", ys, yt.ap[0:NS, 0, :], yt.all(), [], "o_yt0"))

    def xattn(l, subs, psubs, hasS, NT, ntag_, after_sub):
        rmsnorm(Xb, 4 + l, Hb, subs, ntag_)
        QX = Buf(K, "QX", 8, NT, BF16, subs, arena_off=0)
        PTs = [Buf(K, "PT%d" % i, 2, 512, BF16, arena_off=16640 + i * 2048) for i in range(2)]
        RDs_ = [Buf(K, "RD%d" % i, 1, 512, F32, arena_off=20736 + i * 2048) for i in range(2)]
        RDt = [Buf(K, "RDt%d" % i, 1, 512, F32, arena_off=24832 + i * 2048) for i in range(2)]
        KSb = [Buf(K, "KS%d" % i, 2, D, BF16, arena_off=28928 + i * 4096) for i in range(4)]
        VSb = [Buf(K, "VS%d" % i, 2, D, BF16, arena_off=45312 + i * 4096) for i in range(6)]
        PR = Buf(K, "PR", 1, D, F32, arena_off=69888)
        ql = [wget(("xq", l, 0))[0], wget(("xq", l, 1))[0]]
        for c in range(8):
            qt, qk = ql[c // 4]
            for si, (off, n) in enumerate(subs):
                b = nb()
                for k in range(8):
                    mm(PS[b][:, 0:n], qt[:, k, (c % 4) * 128:(c % 4 + 1) * 128], Hb.ap[:, k, off:off + n], k == 0, k == 7, [qk] + Hb.ks(k, off, n), [pk(b)])
                copy(cp_eng(), QX.ap[:, c, off:off + n], PS[b][:, 0:n], [pk(b)], QX.ks(c, off, n), scale=1.0 / 16)
        BO = 6
        bqr = [0]
        if hasS:
            for hf in range(2):
                b = nb()
                qt, qk = ql[hf]
                for k in range(8):
                    mm(PS[b][0:NS, :], Hb.ap[:, k, TP:NT], qt[:, k, :], k == 0, k == 7, [qk] + Hb.ks(k, TP, NS), [pk(b)])
                copy("act", QTK[:, hf * 512:(hf + 1) * 512], PS[b][0:NS, :], [pk(b)], ["QTK"], scale=1.0 / 16)

        def load(s):
            ks_, vs_ = KSb[s % 4], VSb[s % 6]
            dma("pool", ks_.ap, ck[l, s].rearrange("(c p) d -> p c d", p=128), [], ks_.all(), "ks%d" % (s % 4))
            dma("pool", vs_.ap, cv[l, s].rearrange("(c p) d -> p c d", p=128), [], vs_.all(), "vs%d" % (s % 6))

        def front(s):
            ks_ = KSb[s % 4]
            for hf in range(2):
                BQ = 3 + bqr[0] % 3
                bqr[0] += 1
                mm(PS[BQ][:], selb[0:NS, s, :], QTK[0:NS, hf * 512:(hf + 1) * 512], True, True, ["selb", "QTK"], [pk(BQ)])
                for nc_ in range(2):
                    tt(PR.ap[:, 0, nc_ * 512:(nc_ + 1) * 512], ks_.ap[:, nc_, hf * 512:(hf + 1) * 512], PS[BQ][:], ALU.mult, ks_.all() + [pk(BQ)], PR.all())
                P.op("dve", lambda e, hf=hf: e.tensor_reduce(out=SC[:].rearrange("p (n h) -> p n h", h=4)[:, :, 2 * hf:2 * hf + 2],
                                                              in_=PR.ap[:, 0, :].rearrange("p (n h d) -> p n h d", h=2, d=256),
                                                              axis=AX.X, op=ALU.add), PR.all(), ["SC"])
            copy("dve", SCs[:, s, :], SC[:], ["SC"], [("SCs", s)])

        def front_exp(s):
            act(ESs[:, s, :], SCs[:, s, :], AF.Exp, [("SCs", s)], [("ES", s)])

        def back(s):
            vs_ = VSb[s % 6]
            for c in range(8):
                for nc_ in range(2):
                    mm(PS[BO][:, c * NS + s: c * NS + s + 1], vs_.ap[:, nc_, c * 128:(c + 1) * 128], ESs[:, s, nc_ * 4 + c // 2: nc_ * 4 + c // 2 + 1],
                       nc_ == 0, nc_ == 1, vs_.all() + [("ES", s)], [pk(BO)])
            for nc_ in range(2):
                mm(PS[BO][:, 128 + s * 4:128 + (s + 1) * 4], ones1[:], ESs[:, s, nc_ * 4:(nc_ + 1) * 4], nc_ == 0, nc_ == 1, ["ones1", ("ES", s)], [pk(BO)])

        it = 0
        if hasS:
            load(0)
            load(1)
        for h in range(4):
            for si, (off, n) in enumerate(psubs):
                if hasS:
                    if it >= 2:
                        back(2 * it - 4)
                        back(2 * it - 3)
                    if it < 7:
                        load(2 * it + 2)
                        load(2 * it + 3)
                    front(2 * it)
                PT, RD, RT = PTs[it % 2], RDs_[it % 2], RDt[it % 2]
                if hasS:
                    bs_, bd = [0, 1], 2
                else:
                    bs_ = [0, 1] if it % 2 == 0 else [2, 3]
                    bd = 4
                for nc_ in range(2):
                    for dc in range(2):
                        mm(PS[bs_[nc_]][:, 0:n], KTb[:, l, h * 2 + dc, nc_ * 128:(nc_ + 1) * 128], QX.ap[:, h * 2 + dc, off:off + n], dc == 0, dc == 1,
                           [("KTb", l)] + QX.ks(h * 2 + dc, off, n), [pk(bs_[nc_])])
                    act(PT.ap[:, nc_, 0:n], PS[bs_[nc_]][:, 0:n], AF.Exp, [pk(bs_[nc_])], PT.ks(nc_))
                for nc_ in range(2):
                    mm(PS[bd][:, 0:n], ones1[:], PT.ap[:, nc_, 0:n], nc_ == 0, nc_ == 1, ["ones1"] + PT.ks(nc_), [pk(bd)])
                act(RT.ap[:, 0, 0:n], PS[bd][:, 0:n], AF.Ln, [pk(bd)], RT.all())
                act(RD.ap[:, 0, 0:n], RT.ap[:, 0, 0:n], AF.Exp, RT.all(), RD.all(), scale=-1.0)
                for dc in range(2):
                    bo = bs_[dc]
                    for nc_ in range(2):
                        mm(PS[bo][:, 0:n], Vbb[:, l, nc_, h * 256 + dc * 128: h * 256 + (dc + 1) * 128], PT.ap[:, nc_, 0:n], nc_ == 0, nc_ == 1,
                           [("Vbb", l)] + PT.ks(nc_), [pk(bo)])
                    tt(QX.ap[:, h * 2 + dc, off:off + n], PS[bo][:, 0:n], RD.ap[:, 0, 0:n], ALU.mult, [pk(bo)] + RD.all(), QX.ks(h * 2 + dc, off, n))
                if hasS:
                    front(2 * it + 1)
                    front_exp(2 * it)
                    front_exp(2 * it + 1)
                it += 1
        if hasS:
            for s_ in range(12, 16):
                back(s_)
            P.op("dve", lambda e: e.reciprocal(out=RDs[:], in_=PS[BO][:, 128:192].rearrange("p (s h) -> p s h", h=4)), [pk(BO)], ["RDs"])
            for h in range(4):
                tt(QX.ap[:, 2 * h:2 * h + 2, TP:NT], PS[BO][:, 2 * h * NS:(2 * h + 2) * NS].rearrange("p (c s) -> p c s", s=NS),
                   RDs[:, :, h].unsqueeze(1).to_broadcast([128, 2, NS]), ALU.mult, [pk(BO), "RDs"], QX.ks([2 * h, 2 * h + 1], TP, NS))
        ol = [wget(("xo", l, 0))[0], wget(("xo", l, 1))[0]]
        proj_add(ol, 8, QX, subs, 1.0, after_sub)

    def rec(subs, psubs, hasS, NT, ntag_, after_sub):
        rmsnorm(Xb, 3, Hb, subs, ntag_)
        A = 0
        SETS = []
        for q in range(2):
            st_ = []
            for i in range(5):
                st_.append(Buf(K, "SCR%d_%d" % (q, i), 1, 528, F32, arena_off=A)); A += 2112
            st_.append(Buf(K, "KHF%d" % q, 1, 512, BF16, arena_off=A)); A += 1024
            SETS.append(st_)
        QS, SG, LF, BBc, EB, KHF = SETS[0]
        QT = Buf(K, "QT", 4, NT, BF16, subs, arena_off=A); A += 8320
        KT2 = Buf(K, "KT2", 4, NT, BF16, subs, arena_off=A); A += 8320
        GS = Buf(K, "GS", 4, NT, BF16, subs, arena_off=A); A += 8320
        KHT = Buf(K, "KHT", 8, 512, BF16, arena_off=A); A += 8192
        VT = Buf(K, "VT", 8, 512, BF16, arena_off=A); A += 8192
        ATT = Buf(K, "ATT", 1, 256, BF16, arena_off=A); A += 512
        EBL = Buf(K, "EBL", 4, 16, F32, arena_off=A); A += 256
        SSb = [Buf(K, "SS%d" % i, 4, 128, F32, arena_off=A + i * 2048) for i in range(2)]; A += 4096
        SNb = [Buf(K, "SN%d" % i, 4, 128, F32, arena_off=A + i * 2048) for i in range(2)]; A += 4096
        assert A <= ARENA, A

        OSB = [SETS[0][0], SETS[0][1], SETS[1][0], SETS[1][1]]

        def ostage_a(hl, n, src, srckeys):
            copy(cp_eng(), OSB[hl].ap[:, 0, 0:n], src, srckeys, OSB[hl].all())

        def ostage_sq(hl, off, n):
            O2 = SETS[hl % 2][5]
            ob = OSB[hl]
            act(O2.ap[:, 0, 0:n], ob.ap[:, 0, 0:n], AF.Square, ob.all(), O2.all())

        def ostage_b(hl, off, n, do_sq=True):
            q = hl % 2
            A_, B_, C_, O2 = SETS[q][2], SETS[q][3], SETS[q][4], SETS[q][5]
            ob = OSB[hl]
            if do_sq:
                ostage_sq(hl, off, n)
            mm(PS[6][:, 0:n], onesH[:], O2.ap[:, 0, 0:n], True, True, ["onesH"] + O2.all(), [pk(6)])
            act(A_.ap[:, 0, 0:n], PS[6][:, 0:n], AF.Ln, [pk(6), "epsc"], A_.all(), bias=epsc[:])
            act(B_.ap[:, 0, 0:n], A_.ap[:, 0, 0:n], AF.Exp, A_.all(), B_.all(), scale=-0.5)
            tt(C_.ap[:, 0, 0:n], ob.ap[:, 0, 0:n], B_.ap[:, 0, 0:n], ALU.mult, ob.all() + B_.all(), C_.all())
            stt(GS.ap[:, hl, off:off + n], C_.ap[:, 0, 0:n], gon[:], GS.ap[:, hl, off:off + n], ALU.mult, ALU.mult,
                C_.all() + ["gon"] + GS.ks(hl, off, n), GS.ks(hl, off, n))

        pending = []

        for g in range(2):
            (vt_, vk_), = wget(("rv", g))
            for tb in range(8):
                b = nb()
                for k in range(8):
                    mm(PS[b][:], Hb.ap[:, k, tb * 128:(tb + 1) * 128], vt_[:, k, :], k == 0, k == 7, [vk_] + Hb.ks(k, tb * 128, 128), [pk(b)])
                copy(cp_eng(), VT.ap[:, tb, :], PS[b][:], [pk(b)], VT.ks(tb))
            if hasS:
                b = nb()
                for k in range(8):
                    mm(PS[b][0:NS, :], Hb.ap[:, k, TP:NT], vt_[:, k, :], k == 0, k == 7, [vk_] + Hb.ks(k, TP, NS), [pk(b)])
                copy("dve", IST[:, g * 512:(g + 1) * 512], PS[b][0:NS, :], [pk(b)], ["IST"])
            MARKS.append(("rec_g%d_v" % g, sum(1 for o in P.ops if o.eng == "pe")))
            tiles = [([0], 0, 512), ([1, 2], 512, 528)] if hasS else [([0], 0, 512), ([1], 512, 512)]

            def front(hl, h, parts, tile, S):
                QS, SG, LF, BBc, EB, KHF = S
                sis, toff, ntot = tile
                for si in sis:
                    off, n = subs[si]
                    co = off - toff
                    bs = [nb(), nb(), nb()]
                    for pi in range(3):
                        pt, pkey = parts[pi]
                        for k in range(8):
                            mm(PS[bs[pi]][:, 0:n], pt[:, k, 0:128], Hb.ap[:, k, off:off + n], k == 0, k == 7, [pkey] + Hb.ks(k, off, n), [pk(bs[pi])])
                    act(QS.ap[:, 0, co:co + n], PS[bs[0]][:, 0:n], AF.Silu, [pk(bs[0])], QS.all())
                    act(SG.ap[:, 0, co:co + n], PS[bs[1]][:, 0:n], AF.Tanh, [pk(bs[1])], SG.all(), scale=0.5)
                    act(GS.ap[:, hl, off:off + n], PS[bs[2]][:, 0:n], AF.Silu, [pk(bs[2])], GS.ks(hl, off, n))

            def front_b(hl, h, tile, S):
                QS, SG, LF, BBc, EB, KHF = S
                sis, off, n = tile
                P.op("act", lambda e: e.activation(out=LF.ap[:, 0, 0:n], in_=SG.ap[:, 0, 0:n], func=AF.Ln, bias=LBH[:, h:h + 1], scale=HOML[:, h:h + 1]),
                     SG.all() + ["LBH", "HOML"], LF.all())
                P.op("dve", lambda e: e.tensor_scalar(out=SG.ap[:, 0, 0:n], in0=SG.ap[:, 0, 0:n], scalar1=NHOML[:, h:h + 1], scalar2=HOML[:, h:h + 1],
                                                      op0=ALU.mult, op1=ALU.add), SG.all() + LF.all() + ["NHOML", "HOML"], SG.all())
                P.op("dve", lambda e: e.tensor_tensor_scan(out=BBc.ap[:, 0, 0:n], data0=SM[:, off:off + n], data1=LF.ap[:, 0, 0:n], initial=0.0,
                                                           op0=ALU.mult, op1=ALU.add), ["SM"] + LF.all(), BBc.all())

            def back(hl, h, tile, S):
                QS, SG, LF, BBc, EB, KHF = S
                sis, off, n = tile
                si = sis[0]
                act(EB.ap[:, 0, 0:n], BBc.ap[:, 0, 0:n], AF.Exp, BBc.all(), EB.all())
                act(LF.ap[:, 0, 0:n], BBc.ap[:, 0, 0:n], AF.Exp, BBc.all(), LF.all(), scale=-1.0)
                tt(QT.ap[:, hl, off:off + n], QS.ap[:, 0, 0:n], EB.ap[:, 0, 0:n], ALU.mult, QS.all() + EB.all(), QT.ks(hl, off, n))
                tt(LF.ap[:, 0, 0:n], SG.ap[:, 0, 0:n], LF.ap[:, 0, 0:n], ALU.mult, SG.all() + LF.all(), LF.all())
                copy("pool", KT2.ap[:, hl, off:off + n], LF.ap[:, 0, 0:n], LF.all(), KT2.ks(hl, off, n))
                tt(KHF.ap[:, 0, :].rearrange("p (c t) -> p c t", t=64), LF.ap[:, 0, 0:512].rearrange("p (c t) -> p c t", t=64),
                   EB.ap[:, 0, 0:512].rearrange("p (c t) -> p c t", t=64)[:, :, 63:64].to_broadcast([128, 8, 64]), ALU.mult,
                   LF.all() + EB.all(), KHF.all())
                copy("pool", EBL.ap[:, hl, si * 8:(si + 1) * 8], EB.ap[:, 0, 0:512].rearrange("p (c t) -> p c t", t=64)[:, :, 63], EB.all(), EBL.ks(hl))
                if len(sis) == 2:
                    copy("dve", KKs[:, h, :], SG.ap[:, 0, 512:528], SG.all(), [("KKs", h)])
                    copy("dve", QSs[:, h, :], QS.ap[:, 0, 512:528], QS.all(), [("QSs", h)])
                    copy("pool", Fs[:, h, :], EB.ap[:, 0, 512:528], EB.all(), [("Fs", h)])

            def back_tr(hl, h, tile, S):
                QS, SG, LF, BBc, EB, KHF = S
                sis, off, n = tile
                si = sis[0]
                for tbl in range(4):
                    tr(PSB[:, tbl * 128:(tbl + 1) * 128], KHF.ap[:, 0, tbl * 128:(tbl + 1) * 128], identb[:], KHF.all() + ["identb"], ["psb"])
                copy("dve", KHT.ap[:, si * 4:(si + 1) * 4, hl * 128:(hl + 1) * 128], PSB[:, 0:512].rearrange("p (c t) -> p c t", t=128),
                     ["psb"], KHT.ks(range(si * 4, si * 4 + 4)))

            for hl in range(4):
                h = g * 4 + hl
                parts = wget(("rq", h))
                front(hl, h, parts, tiles[0], SETS[0])
                front(hl, h, parts, tiles[1], SETS[1])
                if hl > 0:
                    back_tr(hl - 1, h - 1, tiles[0], SETS[0])
                    back_tr(hl - 1, h - 1, tiles[1], SETS[1])
                front_b(hl, h, tiles[0], SETS[0])
                front_b(hl, h, tiles[1], SETS[1])
                back(hl, h, tiles[0], SETS[0])
                back(hl, h, tiles[1], SETS[1])
            back_tr(3, g * 4 + 3, tiles[0], SETS[0])
            back_tr(3, g * 4 + 3, tiles[1], SETS[1])
            MARKS.append(("rec_g%d_st1" % g, sum(1 for o in P.ops if o.eng == "pe")))
            def sample_step(s, g=g):
                ss, sn = SSb[s % 2], SNb[s % 2]
                dma("sp", ss.ap, srec[s, g * 4:(g + 1) * 4].rearrange("h k v -> k h v"), [], ss.all(), "ss%d" % (s % 2))
                mm(PS[6][:], selb[0:NS, s, :], IST[0:NS, g * 512:(g + 1) * 512], True, True, ["selb", "IST"], [pk(6)])
                for hl in range(4):
                    h = g * 4 + hl
                    P.op("dve", lambda e, hl=hl, h=h, s=s, ss=ss: e.tensor_scalar(out=ss.ap[:, hl, :], in0=ss.ap[:, hl, :], scalar1=Fs[:, h, s:s + 1], scalar2=None, op0=ALU.mult),
                         ss.ks(hl) + [("Fs", h)], ss.ks(hl))
                    stt(sn.ap[:, hl, :], PS[6][:, hl * 128:(hl + 1) * 128], KKs[:, h, s:s + 1], ss.ap[:, hl, :], ALU.mult, ALU.add,
                        [pk(6), ("KKs", h)] + ss.ks(hl), sn.ks(hl))
                for hl in range(4):
                    h = g * 4 + hl
                    mm(PS[4][:, 256 + hl * NS + s: 256 + hl * NS + s + 1], sn.ap[:, hl, :], QSs[:, h, s:s + 1], True, True, sn.ks(hl) + [("QSs", h)], [pk(4)])
                outs.append(dma("sp", rso[s, g * 4:(g + 1) * 4].rearrange("h k v -> k h v"), sn.ap, sn.all(), [], "rs%d" % (s % 2)))

            for tb in range(8):
                for hf in range(2):
                    ch = 2 * tb + hf
                    for hl in range(4):
                        mm(PS[4][hf * 64:(hf + 1) * 64, hl * 64:(hl + 1) * 64], KT2.ap[:, hl, ch * 64:(ch + 1) * 64], QT.ap[:, hl, ch * 64:(ch + 1) * 64], True, True,
                           KT2.ks(hl, ch * 64, 64) + QT.ks(hl, ch * 64, 64), [pk(4)])
                tt(ATT.ap[:, 0, :].rearrange("p (h t) -> p h t", t=64), PS[4][:, 0:256].rearrange("p (h t) -> p h t", t=64),
                   cm[:].unsqueeze(1).to_broadcast([128, 4, 64]), ALU.mult, [pk(4), "cm"], ATT.all())
                for hf in range(2):
                    ch = 2 * tb + hf
                    pb = hf * 64
                    si, cc = ch // 8, ch % 8
                    for hl in range(4):
                        bd_ = 5 + hl // 2
                        mm(PS[bd_][:, (hl % 2) * 128:(hl % 2 + 1) * 128], KHT.ap[pb:pb + 64, tb, hl * 128:(hl + 1) * 128], VT.ap[pb:pb + 64, tb, hl * 128:(hl + 1) * 128], True, True,
                           KHT.ks(tb) + VT.ks(tb), [pk(bd_)])
                    for hl in range(4):
                        h = g * 4 + hl
                        mm(PS[hl][:, cc * 64:(cc + 1) * 64], VT.ap[pb:pb + 64, tb, hl * 128:(hl + 1) * 128], ATT.ap[pb:pb + 64, 0, hl * 64:(hl + 1) * 64], True, False,
                           VT.ks(tb) + ATT.all(), [pk(hl)])
                        mm(PS[hl][:, cc * 64:(cc + 1) * 64], Sbf[:, h, :], QT.ap[:, hl, ch * 64:(ch + 1) * 64], False, True,
                           ["Sbh%d" % h] + QT.ks(hl, ch * 64, 64), [pk(hl)])
                    for hl in range(4):
                        h = g * 4 + hl
                        bd_ = 5 + hl // 2
                        stt(Sf[:, h, :], Sf[:, h, :], EBL.ap[:, hl, ch:ch + 1], PS[bd_][:, (hl % 2) * 128:(hl % 2 + 1) * 128], ALU.mult, ALU.add,
                            ["Sf%d" % h, pk(bd_)] + EBL.ks(hl), ["Sf%d" % h])
                        copy("act", Sbf[:, h, :], Sf[:, h, :], ["Sf%d" % h], ["Sbh%d" % h])
                    if hasS and os.environ.get("KINT", "1") == "1":
                        sample_step(ch)
                    if pending:
                        if cc == 1:
                            ostage_sq(*pending[0]); ostage_sq(*pending[1])
                        elif cc == 2:
                            ostage_b(*pending[0], do_sq=False); ostage_b(*pending[1], do_sq=False)
                            ostage_sq(*pending[2]); ostage_sq(*pending[3])
                        elif cc == 3:
                            ostage_b(*pending[2], do_sq=False); ostage_b(*pending[3], do_sq=False)
                            del pending[:]
                    if cc == 7:
                        off, n = psubs[si]
                        for hl in range(4):
                            ostage_a(hl, n, PS[hl][:, 0:n], [pk(hl)])
                            pending.append((hl, off, n))
            for (hl_, off_, n_) in pending:
                ostage_b(hl_, off_, n_)
            del pending[:]
            if hasS:
                outs.append(dma("sp", rpo[g * 4:(g + 1) * 4].rearrange("h k v -> k h v"), Sf[:, g * 4:(g + 1) * 4, :],
                                ["Sf%d" % (g * 4 + i) for i in range(4)], [], "o_rp"))
                if os.environ.get("KINT", "1") != "1":
                    for s_ in range(NS):
                        sample_step(s_)
                off, n = subs[2]
                for hl in range(4):
                    ostage_a(hl, n, PS[4][:, 256 + hl * NS:256 + (hl + 1) * NS], [pk(4)])
                    ostage_b(hl, off, n)
            MARKS.append(("rec_g%d_st2" % g, sum(1 for o in P.ops if o.eng == "pe")))
            proj_add([wget(("ro", g))[0]], 4, GS, subs, 1.0, after_sub if g == 1 else None)

    try:
        if KSTOP != 0:
            run_pass(0)
            if KSTOP != 10:
                run_pass(1)
                assert wpos[0] == len(sched), (wpos, len(sched))
    except Stop:
        pass
    print("kernel build: ops=%d" % len(P.ops))
    if os.environ.get("KMARKS"):
        import json
        MARKS.append(("end", sum(1 for o in P.ops if o.eng == "pe")))
        json.dump(MARKS, open(os.environ["KMARKS"], "w"))
    P.emit(final_wait_ops=outs)
    return nc


_CACHE = {}


def kernel(**inp):
    if "nc" not in _CACHE:
        _CACHE["nc"] = build_program()
    nc = _CACHE["nc"]
    f = lambda a: np.ascontiguousarray(np.asarray(a, dtype=np.float32))
    wnames = ["norm_ffn1", "w_ffn1_gate", "w_ffn1_up", "w_ffn1_down", "norm_mix", "w_conv_in", "w_conv", "w_conv_out", "lb_raw",
              "w_rec_in", "g_rec_onorm", "w_rec_out", "norm_xattn", "norm_mem", "w_xq", "w_xkv", "w_xo", "norm_ffn2",
              "w_ffn2_gate", "w_ffn2_up", "w_ffn2_down", "norm_final"]
    shared = {n: f(inp[n]) for n in wnames}
    in_maps = []
    for b in range(8):
        m = dict(shared)
        sl = slice(NS * b, NS * (b + 1))
        m["xp"] = f(inp["x_prompt"][b])
        m["xs"] = f(inp["x_sample"][sl, 0])
        m["mem"] = f(inp["mem_prompt"][b])
        m["sconv"] = f(inp["state_conv"][0, sl])
        m["srec"] = f(inp["state_rec"][0, sl])
        m["ck"] = f(inp["cache_mem_k"][:, sl].reshape(2, NS, 256, D))
        m["cv"] = f(inp["cache_mem_v"][:, sl].reshape(2, NS, 256, D))
        in_maps.append(m)
    res = run_bass_kernel_spmd(nc, in_maps, core_ids=list(range(8)))
    R = res.results
    y_prompt = np.stack([R[b]["yp"] for b in range(8)])
    y_sample = np.concatenate([R[b]["ys"] for b in range(8)])[:, None, :]
    mk_ = np.stack([R[b]["mk"] for b in range(8)], axis=1).reshape(2, 8, 256, 4, 256)
    mv_ = np.stack([R[b]["mv"] for b in range(8)], axis=1).reshape(2, 8, 256, 4, 256)
    conv_p = np.stack([R[b]["cp"] for b in range(8)])[None]
    rec_p = np.stack([R[b]["rp"] for b in range(8)])[None]
    conv_s = np.concatenate([R[b]["cs"] for b in range(8)])[None]
    rec_s = np.concatenate([R[b]["rs"] for b in range(8)])[None]
    return (y_prompt.astype(np.float32), y_sample.astype(np.float32), mk_.astype(np.float32), mv_.astype(np.float32),
            conv_p.astype(np.float32), rec_p.astype(np.float32), conv_s.astype(np.float32), rec_s.astype(np.float32))
```

```python
import contextlib
import numpy as np
import concourse.bass as bass
import concourse.mybir as mybir
from concourse.bass_utils import run_bass_kernel_spmd

F32 = mybir.dt.float32
BF16 = mybir.dt.bfloat16
AF = mybir.ActivationFunctionType
ALU = mybir.AluOpType
AX = mybir.AxisListType
COMPUTE = ("pe", "act", "dve", "pool")
D = 1024
DFF = 2816
TP = 1024
NS = 16
EPS = 1e-6
NSLOT = 3


class Op:
    __slots__ = ("idx", "eng", "fn", "deps", "raw", "is_dma", "semkey", "signal", "seq", "wk")


class Prog:
    def __init__(self, nc):
        self.nc = nc
        self.ops = []
        self.last_writer = {}
        self.readers = {}
        self.span = {}
        self.blocks = {}
        self.h = {"pe": nc.tensor, "act": nc.scalar, "dve": nc.vector, "pool": nc.gpsimd, "sp": nc.sync}

    def op(self, eng, fn, reads=(), writes=(), dma=False, semkey=None):
        o = Op()
        o.idx = len(self.ops)
        o.eng, o.fn, o.is_dma, o.semkey, o.signal, o.seq = eng, fn, dma, semkey, False, None
        deps = set()
        lw, rd = self.last_writer, self.readers
        xw = [k for k in reads if k == "psb" or (isinstance(k, tuple) and k[0] == "ps")]
        if xw:
            writes = list(writes) + [k for k in xw if k not in writes]
        raw = set()
        for k in reads:
            w = lw.get(k)
            if w is not None:
                deps.add(w)
                raw.add(w)
        o.raw = raw
        o.wk = list(writes)
        for k in writes:
            w = lw.get(k)
            if w is not None:
                deps.add(w)
            deps.update(rd.get(k, ()))
            sp = self.span.get(k)
            if sp is not None:
                a, b = sp
                seen = set()
                for blk in range(a // 256, (b - 1) // 256 + 1):
                    st_ = self.blocks.get(blk)
                    if st_ is None:
                        st_ = self.blocks[blk] = set()
                    for k2 in st_:
                        if k2 == k or k2 in seen:
                            continue
                        seen.add(k2)
                        a2, b2 = self.span[k2]
                        if a2 < b and a < b2:
                            w2 = lw.get(k2)
                            if w2 is not None:
                                deps.add(w2)
                            deps.update(rd.get(k2, ()))
                    if blk * 256 >= a and (blk + 1) * 256 <= b:
                        st_.clear()
                    st_.add(k)
        deps.discard(o.idx)
        o.deps = deps
        for k in reads:
            rd.setdefault(k, []).append(o.idx)
        for k in writes:
            lw[k] = o.idx
            rd[k] = []
        self.ops.append(o)
        return o

    def emit(self, final_wait_ops=()):
        nc, ops = self.nc, self.ops
        needed = []
        for o in ops:
            best = {}
            for d in o.deps:
                p = ops[d]
                if p.is_dma or o.is_dma or p.eng != o.eng or p.eng != "pe":
                    sk = ("dma", p.semkey) if p.is_dma else ("eng", p.eng)
                    if d > best.get(sk, -1):
                        best[sk] = d
            nd = list(best.values())
            needed.append(nd)
            for d in nd:
                ops[d].signal = True
        for d in final_wait_ops:
            d.signal = True
        eng_cnt = {e: 0 for e in COMPUTE}
        dma_cnt = {}
        for o in ops:
            if o.is_dma:
                dma_cnt[o.semkey] = dma_cnt.get(o.semkey, 0) + 16
                o.seq = dma_cnt[o.semkey]
            elif o.signal:
                eng_cnt[o.eng] += 1
                o.seq = eng_cnt[o.eng]
        for o in ops:
            if o.is_dma and isinstance(o.semkey, str) and o.semkey.startswith("G:"):
                o.seq = dma_cnt[o.semkey]
        sems = {}
        stack = contextlib.ExitStack()
        for e in COMPUTE:
            sems[("eng", e)] = stack.enter_context(nc.semaphore("s_" + e))
        for i, k in enumerate(dma_cnt):
            sems[("dma", k)] = stack.enter_context(nc.semaphore("d%d" % i))
        waited = {}
        for o in ops:
            h = self.h[o.eng]
            w = {}
            for d in needed[o.idx]:
                p = ops[d]
                sk = ("dma", p.semkey) if p.is_dma else ("eng", p.eng)
                if p.seq > w.get(sk, 0):
                    w[sk] = p.seq
            for sk, v in w.items():
                if waited.get((o.eng, sk), 0) >= v:
                    continue
                waited[(o.eng, sk)] = v
                h.wait_ge(sems[sk], v)
            ins = o.fn(h)
            if o.is_dma:
                ins.then_inc(sems[("dma", o.semkey)], 16)
            elif o.signal:
                ins.then_inc(sems[("eng", o.eng)], 1)
        w = {}
        for p in final_wait_ops:
            sk = ("dma", p.semkey) if p.is_dma else ("eng", p.eng)
            if p.seq > w.get(sk, 0):
                w[sk] = p.seq
        for sk, v in w.items():
            nc.sync.wait_ge(sems[sk], v)
        return stack


class Buf:
    def __init__(self, K, name, C, T, dtype, subs=None, arena_off=None, tensor=None):
        self.K, self.name, self.C, self.T = K, name, C, T
        self.es = 2 if dtype == BF16 else 4
        self.subs = subs if subs is not None else [(0, T)]
        self.off = arena_off
        if tensor is not None:
            self.ap = tensor[:]
        else:
            nb = C * T * self.es
            a = arena_off // 4
            b = (arena_off + nb + 3) // 4
            v = K.AR[:, a:b]
            if dtype == BF16:
                v = v.bitcast(BF16)
            v = v[:, 0:C * T]
            self.ap = v.rearrange("p (c t) -> p c t", t=T)

    def k(self, c, si):
        key = (self.name, c, si, self.T, self.off)
        if self.off is not None:
            o, n = self.subs[si]
            s0 = self.off + (c * self.T + o) * self.es
            self.K.P.span[key] = (s0, s0 + n * self.es)
        return key

    def ks(self, cs, t0=0, n=None):
        if n is None:
            n = self.T - t0
        if isinstance(cs, int):
            cs = [cs]
        out = []
        for si, (o, m) in enumerate(self.subs):
            if o < t0 + n and t0 < o + m:
                for c in cs:
                    out.append(self.k(c, si))
        return out

    def all(self):
        return self.ks(range(self.C))


class KB:
    pass


class Stop(Exception):
    pass


import os
KSTOP = int(os.environ.get("KSTOP", "-1"))


def build_program():
    nc = bass.Bass("TRN2", target_bir_lowering=False)
    K = KB()
    P = Prog(nc)
    K.P = P
    st = contextlib.ExitStack()
    din = lambda n, s: nc.dram_tensor(n, s, F32, kind="ExternalInput").ap()
    dout = lambda n, s: nc.dram_tensor(n, s, F32, kind="ExternalOutput").ap()
    xp = din("xp", [2048, D]); xs = din("xs", [NS, D]); mem = din("mem", [256, D])
    sconv = din("sconv", [NS, 2, D]); srec = din("srec", [NS, 8, 128, 128])
    ck = din("ck", [2, NS, 256, D]); cv = din("cv", [2, NS, 256, D])
    n_ffn1 = din("norm_ffn1", [2, D]); wg1 = din("w_ffn1_gate", [2, D, DFF]); wu1 = din("w_ffn1_up", [2, D, DFF]); wd1 = din("w_ffn1_down", [2, DFF, D])
    n_mix = din("norm_mix", [2, D]); wci = din("w_conv_in", [1, D, 3 * D]); wcv = din("w_conv", [1, 3, D]); wco = din("w_conv_out", [1, D, D])
    lbr = din("lb_raw", [2, D]); wri = din("w_rec_in", [1, D, 4 * D]); gro = din("g_rec_onorm", [1, 128]); wro = din("w_rec_out", [1, D, D])
    n_xa = din("norm_xattn", [2, D]); n_mem = din("norm_mem", [2, D]); wxq = din("w_xq", [2, D, D]); wxkv = din("w_xkv", [2, D, 2 * D]); wxo = din("w_xo", [2, D, D])
    n_ffn2 = din("norm_ffn2", [2, D]); wg2 = din("w_ffn2_gate", [2, D, DFF]); wu2 = din("w_ffn2_up", [2, D, DFF]); wd2 = din("w_ffn2_down", [2, DFF, D])
    n_fin = din("norm_final", [D])
    yp = dout("yp", [2048, D]); ys = dout("ys", [NS, D]); mk = dout("mk", [2, 256, D]); mv = dout("mv", [2, 256, D])
    cpo = dout("cp", [2, D]); rpo = dout("rp", [8, 128, 128]); cso = dout("cs", [NS, 2, D]); rso = dout("rs", [NS, 8, 128, 128])

    sbt = lambda n, s, dt=F32: st.enter_context(nc.sbuf_tensor(n, s, dt))
    NTB = TP + NS
    SUBB = [(0, 512), (512, 512), (TP, NS)]
    Xb = Buf(K, "X", 8, NTB, F32, SUBB, tensor=sbt("X", [128, 8, NTB]))
    Hb = Buf(K, "H", 8, NTB, BF16, SUBB, tensor=sbt("H", [128, 8, NTB], BF16))
    WS = [sbt("ws%d" % i, [128, 4096], BF16) for i in range(NSLOT)]
    KTb = sbt("KTb", [128, 2, 8, 256], BF16)
    Vbb = sbt("Vbb", [128, 2, 2, D], BF16)
    VC = sbt("VC", [128, 8, 16])
    LBt = sbt("LB", [128, 8]); OML = sbt("OML", [128, 8]); NOML = sbt("NOML", [128, 8]); DLT = sbt("DLT", [128, 8])
    HOML = sbt("HOML", [128, 8]); NHOML = sbt("NHOML", [128, 8]); LBH = sbt("LBH", [128, 8])
    gon = sbt("gon", [128, 1]); epsc = sbt("epsc", [128, 1])
    identf = sbt("identf", [128, 128]); identb = sbt("identb", [128, 128], BF16)
    onesD = sbt("onesD", [128, 128], BF16); ones1 = sbt("ones1", [128, 128], BF16); onesH = sbt("onesH", [128, 128], BF16)
    cm = sbt("cm", [128, 64]); SM = sbt("SM", [128, NTB])
    selb = sbt("selb", [16, 16, 128], BF16)
    Sf = sbt("Sf", [128, 8, 128]); Sbf = sbt("Sbf", [128, 8, 128], BF16)
    convst = sbt("convst", [128, 8, 2])
    SQ = sbt("SQ", [128, 8, 512], BF16)
    RSa = sbt("RSa", [128, 512]); RSb = sbt("RSb", [128, 512])
    SGs = [sbt("SG%d" % i, [128, 512]) for i in range(2)]
    ST0 = sbt("ST0", [128, 8, NS]); ST1 = sbt("ST1", [128, 8, NS]); NV = sbt("NV", [128, 8, NS])
    QTK = sbt("QTK", [NS, D], BF16); IST = sbt("IST", [NS, D], BF16)
    KKs = sbt("KKs", [128, 8, NS]); Fs = sbt("Fs", [128, 8, NS]); QSs = sbt("QSs", [128, 8, NS])
    ESs = sbt("ESs", [128, NS, 8], BF16); SC = sbt("SC", [128, 8]); RDs = sbt("RDs", [128, NS, 4]); SCs = sbt("SCs", [128, NS, 8])
    ARENA = 74 * 1024
    K.AR = sbt("AR", [128, ARENA // 4])
    PS = [st.enter_context(nc.psum_tensor("ps%d" % i, [128, 512], F32)) for i in range(7)]
    PSB = st.enter_context(nc.psum_tensor("psb", [128, 1024], BF16))
    pk = lambda b: ("ps", b)
    rr = [0]

    def nb():
        b = rr[0]
        rr[0] = (b + 1) % 7
        return b

    flip = [0]

    def cp_eng():
        flip[0] ^= 1
        return "act" if flip[0] else "dve"

    def copy(eng, out, in_, reads, writes, scale=None):
        if eng == "act":
            sc_ = 1.0 if scale is None else scale
            P.op("act", lambda e: e.activation(out=out, in_=in_, func=AF.Copy, scale=sc_), reads, writes)
        else:
            if scale is None:
                P.op(eng, lambda e: e.tensor_copy(out=out, in_=in_), reads, writes)
            else:
                P.op(eng, lambda e: e.tensor_scalar(out=out, in0=in_, scalar1=scale, scalar2=None, op0=ALU.mult), reads, writes)

    def mm(out, lhsT, rhs, start, stop, reads, writes):
        P.op("pe", lambda e: e.matmul(out, lhsT=lhsT, rhs=rhs, start=start, stop=stop), reads, writes)

    def tr(out, in_, ident, reads, writes):
        P.op("pe", lambda e: e.transpose(out=out, in_=in_, identity=ident), reads, writes)

    def act(out, in_, func, reads, writes, bias=None, scale=1.0):
        if bias is None:
            P.op("act", lambda e: e.activation(out=out, in_=in_, func=func, scale=scale), reads, writes)
        else:
            P.op("act", lambda e: e.activation(out=out, in_=in_, func=func, bias=bias, scale=scale), reads, writes)

    def tt(out, in0, in1, op, reads, writes, eng="dve"):
        P.op(eng, lambda e: e.tensor_tensor(out=out, in0=in0, in1=in1, op=op), reads, writes)

    def stt(out, in0, scalar, in1, op0, op1, reads, writes):
        P.op("dve", lambda e: e.scalar_tensor_tensor(out=out, in0=in0, scalar=scalar, in1=in1, op0=op0, op1=op1), reads, writes)

    def dma(eng, out, in_, reads, writes, semkey):
        return P.op(eng, lambda e: e.dma_start(out=out, in_=in_), reads, writes, dma=True, semkey=semkey)

    outs = []
    MARKS = []
    K.MARKS = MARKS

    wv = lambda w: w.rearrange("(c p) n -> p c n", p=128)
    sched = []

    def U(name, parts):
        sched.append((name, parts))

    def sched_ffn(l, wg, wu, wd, tag):
        for g in range(11):
            U((tag, "gu", l, g), [(wv(wg[l])[:, :, g * 256:(g + 1) * 256], 8, 256), (wv(wu[l])[:, :, g * 256:(g + 1) * 256], 8, 256)])
        for c in range(8):
            U((tag, "d", l, c), [(wv(wd[l])[:, :, c * 128:(c + 1) * 128], 22, 128)])

    def sched_xattn(l):
        for hf in range(2):
            U(("xq", l, hf), [(wv(wxq[l])[:, :, hf * 512:(hf + 1) * 512], 8, 512)])
        for hf in range(2):
            U(("xo", l, hf), [(wv(wxo[l])[:, :, hf * 512:(hf + 1) * 512], 8, 512)])

    for l in range(2):
        for hf in range(2):
            U(("xk", l, hf), [(wv(wxkv[l])[:, :, hf * 512:(hf + 1) * 512], 8, 512)])
        for hf in range(2):
            U(("xv", l, hf), [(wv(wxkv[l])[:, :, D + hf * 512:D + (hf + 1) * 512], 8, 512)])
    for pas in range(2):
        sched_ffn(0, wg1, wu1, wd1, "f1")
        for c in range(8):
            U(("ci", c), [(wv(wci[0])[:, :, pt * D + c * 128: pt * D + (c + 1) * 128], 8, 128) for pt in range(3)])
        for hf in range(2):
            U(("co", hf), [(wv(wco[0])[:, :, hf * 512:(hf + 1) * 512], 8, 512)])
        sched_xattn(0)
        sched_ffn(0, wg2, wu2, wd2, "f2")
        sched_ffn(1, wg1, wu1, wd1, "f1")
        for g in range(2):
            U(("rv", g), [(wv(wri[0])[:, :, 2 * D + g * 512: 2 * D + (g + 1) * 512], 8, 512)])
            for hl in range(4):
                hh = g * 4 + hl
                U(("rq", hh), [(wv(wri[0])[:, :, pt * D + hh * 128: pt * D + (hh + 1) * 128], 8, 128) for pt in (0, 1, 3)])
            U(("ro", g), [(wro[0][g * 512:(g + 1) * 512, :].rearrange("(c p) n -> p c n", p=128), 4, D)])
        sched_xattn(1)
        sched_ffn(1, wg2, wu2, wd2, "f2")
    wpos = [0, 0]

    def issue_upto(i):
        while wpos[1] <= i and wpos[1] < len(sched):
            j = wpos[1]
            slot = j % NSLOT
            off = 0
            first = None
            for pi, (src, kc, n) in enumerate(sched[j][1]):
                dst = WS[slot][:, off:off + kc * n].rearrange("p (c n) -> p c n", n=n)
                if pi == 0:
                    first = dma("pool", dst, src, [], [("ws", slot, q) for q in range(3)], "w%d_%d" % (slot, pi))
                else:
                    o = dma("pool", dst, src, [], [], "w%d_%d" % (slot, pi))
                    o.deps = set(first.deps)
                    P.last_writer[("ws", slot, pi)] = o.idx
                    P.readers[("ws", slot, pi)] = []
                off += kc * n
            wpos[1] += 1

    def wget(name):
        i = wpos[0]
        assert sched[i][0] == name, (sched[i][0], name)
        issue_upto(i + NSLOT - 2)
        wpos[0] += 1
        slot = i % NSLOT
        res = []
        off = 0
        for pi, (src, kc, n) in enumerate(sched[i][1]):
            res.append((WS[slot][:, off:off + kc * n].rearrange("p (c n) -> p c n", n=n), ("ws", slot, pi)))
            off += kc * n
        return res

    P.op("pool", lambda e: e.memset(identf[:], 1.0), [], ["identf"])
    P.op("pool", lambda e: e.affine_select(out=identf[:], in_=identf[:], pattern=[[-1, 128]], compare_op=ALU.is_equal, fill=0.0,
                                           base=0, channel_multiplier=1), ["identf"], ["identf"])
    copy("dve", identb[:], identf[:], ["identf"], ["identb"])
    P.op("dve", lambda e: e.memset(onesD[:], 1.0 / D), [], ["onesD"])
    P.op("dve", lambda e: e.memset(ones1[:], 1.0), [], ["ones1"])
    P.op("dve", lambda e: e.memset(onesH[:], 1.0 / 128), [], ["onesH"])
    P.op("dve", lambda e: e.memset(epsc[:], EPS), [], ["epsc"])
    P.op("dve", lambda e: e.memset(Sf[:], 0.0), [], ["Sf%d" % h for h in range(8)])
    P.op("dve", lambda e: e.memset(Sbf[:], 0.0), [], ["Sbh%d" % h for h in range(8)])
    P.op("dve", lambda e: e.memset(convst[:], 0.0), [], ["convst%d" % c for c in range(8)])
    P.op("dve", lambda e: e.memset(SM[:], 1.0), [], ["SM"])
    P.op("dve", lambda e: e.memset(SM[:, 0:TP].rearrange("p (c t) -> p c t", t=64)[:, :, 0:1], 0.0), ["SM"], ["SM"])
    P.op("dve", lambda e: e.memset(SM[:, TP:NTB], 0.0), ["SM"], ["SM"])
    P.op("pool", lambda e: e.memset(cm[:], 1.0), [], ["cm"])
    for hf in range(2):
        P.op("pool", lambda e, hf=hf: e.affine_select(out=cm[hf * 64:(hf + 1) * 64, :], in_=cm[hf * 64:(hf + 1) * 64, :], pattern=[[1, 64]],
                                                      compare_op=ALU.is_ge, fill=0.0, base=0, channel_multiplier=-1), ["cm"], ["cm"])
    P.op("pool", lambda e: e.memset(selb[:], 1.0), [], ["selb"])
    P.op("pool", lambda e: e.affine_select(out=selb[:], in_=selb[:], pattern=[[-1, 16], [0, 128]], compare_op=ALU.is_equal, fill=0.0,
                                           base=0, channel_multiplier=1), ["selb"], ["selb"])
    rows = [n_ffn1[0:1, :], n_ffn1[1:2, :], n_mix[0:1, :], n_mix[1:2, :], n_xa[0:1, :], n_xa[1:2, :], n_mem[0:1, :], n_mem[1:2, :],
            n_ffn2[0:1, :], n_ffn2[1:2, :], n_fin.rearrange("(o d) -> o d", o=1), wcv[0, 0:1, :], wcv[0, 1:2, :], wcv[0, 2:3, :],
            lbr[0:1, :], lbr[1:2, :]]
    VRb = Buf(K, "VR", 1, D, F32, arena_off=40960)
    VR = VRb.ap[:, 0, :]
    for r, src in enumerate(rows):
        dma("sp", VR[r:r + 1, :], src, [], [("VRr", r)], "G:c")
    b = nb()
    for c in range(8):
        tr(PS[b][:, c * 16:(c + 1) * 16], VR[0:16, c * 128:(c + 1) * 128], identf[0:16, 0:16], ["identf"] + [("VRr", r) for r in range(16)], [pk(b)])
    copy("dve", VC[:], PS[b][:, 0:128].rearrange("p (c r) -> p c r", r=16), [pk(b)], ["VC"])
    P.op("sp", lambda e: e.dma_start(out=gon[:], in_=gro.rearrange("o p -> p o")), [], ["gon"], dma=True, semkey="G:c")
    tt(DLT[:], VC[:, :, 15], VC[:, :, 14], ALU.subtract, ["VC"], ["DLT"])
    act(LBt[:], DLT[:], AF.Sigmoid, ["DLT"], ["LB"])
    act(OML[:], DLT[:], AF.Sigmoid, ["DLT"], ["OML"], scale=-1.0)
    P.op("dve", lambda e: e.tensor_scalar(out=NOML[:], in0=OML[:], scalar1=-1.0, scalar2=None, op0=ALU.mult), ["OML"], ["NOML"])
    P.op("dve", lambda e: e.tensor_scalar(out=HOML[:], in0=OML[:], scalar1=0.5, scalar2=None, op0=ALU.mult), ["OML"], ["HOML"])
    P.op("dve", lambda e: e.tensor_scalar(out=NHOML[:], in0=OML[:], scalar1=-0.5, scalar2=None, op0=ALU.mult), ["OML"], ["NHOML"])
    tt(LBH[:], LBt[:], HOML[:], ALU.add, ["LB", "HOML"], ["LBH"])
    gc = lambda r, c: VC[:, c, r:r + 1]

    if KSTOP == -2:
        outs.append(dma("sp", ys, SM[0:16, 0:1024], ["VC", "gon", "LB", "OML", "NOML", "selb", "cm", "SM", "identb"], [], "o_dbg"))
        P.emit(final_wait_ops=outs)
        return nc
    norm_done = set()

    def norm_sub(src, grow, dst, subs, si, tag, part="ab"):
        off, n = subs[si]
        if "a" in part:
            if (tag, si) in norm_done:
                return
            norm_done.add((tag, si))
            act(SQ[:, :, 0:n], src.ap[:, :, off:off + n], AF.Square, src.ks(range(8), off, n), ["SQ"])
        if "b" not in part:
            return
        if (tag, si, "b") in norm_done:
            return
        norm_done.add((tag, si, "b"))
        b = nb()
        for c in range(8):
            mm(PS[b][:, 0:n], onesD[:], SQ[:, c, 0:n], c == 0, c == 7, ["onesD", "SQ"], [pk(b)])
        act(RSa[:, 0:n], PS[b][:, 0:n], AF.Ln, [pk(b), "epsc"], ["RSa"], bias=epsc[:])
        act(RSb[:, 0:n], RSa[:, 0:n], AF.Exp, ["RSa"], ["RSb"], scale=-0.5)
        for c in range(8):
            stt(dst.ap[:, c, off:off + n], src.ap[:, c, off:off + n], gc(grow, c), RSb[:, 0:n], ALU.mult, ALU.mult,
                src.ks(c, off, n) + ["VC", "RSb"], dst.ks(c, off, n))

    ntag = [0]

    def rmsnorm(src, grow, dst, subs, tag=None):
        if tag is None:
            ntag[0] += 1
            tag = ("anon", ntag[0])
        for si in range(len(subs)):
            norm_sub(src, grow, dst, subs, si, tag)

    def proj_add(wlist, nk, src, subs, scale, after_sub=None):
        per = 8 // len(wlist)
        for si, (off, n) in enumerate(subs):
            for c in range(8):
                wt, wkey = wlist[c // per]
                c0 = (c % per) * 128
                b = nb()
                for k in range(nk):
                    mm(PS[b][:, 0:n], wt[:, k, c0:c0 + 128], src.ap[:, k, off:off + n], k == 0, k == nk - 1,
                       [wkey] + src.ks(k, off, n), [pk(b)])
                stt(Xb.ap[:, c, off:off + n], PS[b][:, 0:n], scale, Xb.ap[:, c, off:off + n], ALU.mult, ALU.add,
                    [pk(b)] + Xb.ks(c, off, n), Xb.ks(c, off, n))
                if after_sub is not None and si > 0 and c == 0:
                    after_sub(si - 1, "b")
            if after_sub is not None:
                after_sub(si, "a")
        if after_sub is not None:
            after_sub(len(subs) - 1, "b")

    def ffn(l, which, subs, ntag_, after_sub=None):
        tag = "f1" if which == 1 else "f2"
        rmsnorm(Xb, (l if which == 1 else 8 + l), Hb, subs, ntag_)
        NT = subs[-1][0] + subs[-1][1]
        Ub = Buf(K, "U", 22, NT, BF16, subs, arena_off=0)
        r = 0
        for g in range(11):
            (gt, gk), (ut, uk) = wget((tag, "gu", l, g))
            for jj in range(2):
                j = g * 2 + jj
                for si, (off, n) in enumerate(subs):
                    bg, bu = nb(), nb()
                    for k in range(8):
                        mm(PS[bg][:, 0:n], gt[:, k, jj * 128:(jj + 1) * 128], Hb.ap[:, k, off:off + n], k == 0, k == 7, [gk] + Hb.ks(k, off, n), [pk(bg)])
                    for k in range(8):
                        mm(PS[bu][:, 0:n], ut[:, k, jj * 128:(jj + 1) * 128], Hb.ap[:, k, off:off + n], k == 0, k == 7, [uk] + Hb.ks(k, off, n), [pk(bu)])
                    sg = SGs[r % 2]
                    sgk = "SG%d" % (r % 2)
                    r += 1
                    act(sg[:, 0:n], PS[bg][:, 0:n], AF.Silu, [pk(bg)], [sgk])
                    tt(Ub.ap[:, j, off:off + n], sg[:, 0:n], PS[bu][:, 0:n], ALU.mult, [sgk, pk(bu)], Ub.ks(j, off, n))
        for c2 in range(4):
            dl = [wget((tag, "d", l, c2 * 2))[0], wget((tag, "d", l, c2 * 2 + 1))[0]]
            if c2 < 3:
                order = [(si, cc) for cc in range(2) for si in range(len(subs))]
            else:
                order = [(si, cc) for si in range(len(subs)) for cc in range(2)]
            for si, cc in order:
                off, n = subs[si]
                c = c2 * 2 + cc
                dt_, dk = dl[cc]
                b = nb()
                for j in range(22):
                    mm(PS[b][:, 0:n], dt_[:, j, 0:128], Ub.ap[:, j, off:off + n], j == 0, j == 21, [dk] + Ub.ks(j, off, n), [pk(b)])
                stt(Xb.ap[:, c, off:off + n], PS[b][:, 0:n], 0.5, Xb.ap[:, c, off:off + n], ALU.mult, ALU.add,
                    [pk(b)] + Xb.ks(c, off, n), Xb.ks(c, off, n))
                if c2 == 3 and after_sub is not None:
                    if cc == 0 and si > 0:
                        after_sub(si - 1, "b")
                    if cc == 1:
                        after_sub(si, "a")
        if after_sub is not None:
            after_sub(len(subs) - 1, "b")

    def to_tok(src_fn, ncols, dst_tile, dst_key, reads):
        for hf in range(2):
            b = nb()
            for cc in range(4):
                c = hf * 4 + cc
                tr(PS[b][0:ncols, cc * 128:(cc + 1) * 128], src_fn(c), identf[:], reads + ["identf"], [pk(b)])
            copy(cp_eng(), dst_tile[0:ncols, hf * 512:(hf + 1) * 512], PS[b][0:ncols, :], [pk(b)], [dst_key])

    XTb = [Buf(K, "XT%d" % i, 1, D, F32, arena_off=45056 + i * 4096) for i in range(2)]
    xpre = set()

    def prefetch_x(pas):
        for tb in range(2):
            dma("sp", XTb[tb].ap[:, 0, :], xp[pas * TP + tb * 128: pas * TP + (tb + 1) * 128, :], [], XTb[tb].all(), "xt%d" % tb)
            xpre.add((pas, tb))

    MT = Buf(K, "MT", 2, D, F32, arena_off=0)
    MX = Buf(K, "MX", 8, 256, F32, arena_off=8192)
    MH = Buf(K, "MH", 8, 256, BF16, arena_off=16384)
    OK_ = Buf(K, "OK", 2, D, F32, arena_off=20480)
    OV_ = Buf(K, "OV", 2, D, F32, arena_off=28672)
    dma("sp", MT.ap, mem.rearrange("(c p) d -> p c d", p=128), [], MT.all(), "G:c")
    prefetch_x(0)
    for nc_ in range(2):
        for hf in range(2):
            b = nb()
            for cc in range(4):
                c = hf * 4 + cc
                tr(PS[b][:, cc * 128:(cc + 1) * 128], MT.ap[:, nc_, c * 128:(c + 1) * 128], identf[:], MT.all() + ["identf"], [pk(b)])
            copy(cp_eng(), MX.ap[:, hf * 4:(hf + 1) * 4, nc_ * 128:(nc_ + 1) * 128], PS[b][:].rearrange("p (c t) -> p c t", t=128),
                 [pk(b)], MX.ks(range(hf * 4, hf * 4 + 4)))
    for l in range(2):
        rmsnorm(MX, 6 + l, MH, [(0, 256)])
        ktl = [wget(("xk", l, 0))[0], wget(("xk", l, 1))[0]]
        for dc in range(8):
            b = nb()
            kt, kk_ = ktl[dc // 4]
            for k in range(8):
                mm(PS[b][:, 0:256], kt[:, k, (dc % 4) * 128:(dc % 4 + 1) * 128], MH.ap[:, k, :], k == 0, k == 7, [kk_] + MH.ks(k), [pk(b)])
            copy(cp_eng(), KTb[:, l, dc, :], PS[b][:, 0:256], [pk(b)], [("KTb", l)])
        for nc_ in range(2):
            for hf in range(2):
                b = nb()
                kt, kk_ = ktl[hf]
                for k in range(8):
                    mm(PS[b][:], MH.ap[:, k, nc_ * 128:(nc_ + 1) * 128], kt[:, k, :], k == 0, k == 7, [kk_] + MH.ks(k), [pk(b)])
                copy(cp_eng(), OK_.ap[:, nc_, hf * 512:(hf + 1) * 512], PS[b][:], [pk(b)], OK_.all())
        outs.append(dma("sp", mk[l].rearrange("(c p) d -> p c d", p=128), OK_.ap, OK_.all(), [], "o_ok"))
        vtl = [wget(("xv", l, 0))[0], wget(("xv", l, 1))[0]]
        for nc_ in range(2):
            for hf in range(2):
                b = nb()
                vt, vk_ = vtl[hf]
                for k in range(8):
                    mm(PS[b][:], MH.ap[:, k, nc_ * 128:(nc_ + 1) * 128], vt[:, k, :], k == 0, k == 7, [vk_] + MH.ks(k), [pk(b)])
                copy("act", OV_.ap[:, nc_, hf * 512:(hf + 1) * 512], PS[b][:], [pk(b)], OV_.all())
                copy("dve", Vbb[:, l, nc_, hf * 512:(hf + 1) * 512], PS[b][:], [pk(b)], [("Vbb", l)])
        outs.append(dma("sp", mv[l].rearrange("(c p) d -> p c d", p=128), OV_.ap, OV_.all(), [], "o_ov"))

    def run_pass(pas):
        hasS = pas == 1
        subs = SUBB if hasS else SUBB[:2]
        psubs = SUBB[:2]
        NT = NTB if hasS else TP
        t0 = pas * TP
        XT = XTb
        for tb in range(8):
            xt = XT[tb % 2]
            if (pas, tb) not in xpre:
                dma("sp", xt.ap[:, 0, :], xp[t0 + tb * 128: t0 + (tb + 1) * 128, :], [], xt.all(), "xt%d" % (tb % 2))
            for hf in range(2):
                b = nb()
                for cc in range(4):
                    c = hf * 4 + cc
                    tr(PS[b][:, cc * 128:(cc + 1) * 128], xt.ap[:, 0, c * 128:(c + 1) * 128], identf[:], xt.all() + ["identf"], [pk(b)])
                copy(cp_eng(), Xb.ap[:, hf * 4:(hf + 1) * 4, tb * 128:(tb + 1) * 128], PS[b][:].rearrange("p (c t) -> p c t", t=128),
                     [pk(b)], Xb.ks(range(hf * 4, hf * 4 + 4), tb * 128, 128))
        if hasS:
            TOKb = Buf(K, "TOK", 1, D, F32, arena_off=8192)
            SCTb = Buf(K, "SCT", 1, 2 * D, F32, arena_off=12288)
            TOK = TOKb.ap[0:NS, 0, :]
            SCT = SCTb.ap[0:NS, 0, :]
            dma("sp", TOK, xs, [], TOKb.all(), "G:s")
            b = nb()
            for c in range(8):
                tr(PS[b][:, c * 16:(c + 1) * 16], TOK[0:16, c * 128:(c + 1) * 128], identf[0:16, 0:16], TOKb.all() + ["identf"], [pk(b)])
            copy("dve", Xb.ap[:, :, TP:NTB], PS[b][:, 0:128].rearrange("p (c r) -> p c r", r=16), [pk(b)], Xb.ks(range(8), TP, NS))
            dma("sp", SCT, sconv.rearrange("s j d -> s (j d)"), [], SCTb.all(), "G:s")
            for j, STj in enumerate((ST0, ST1)):
                b = nb()
                for c in range(8):
                    tr(PS[b][:, c * 16:(c + 1) * 16], SCT[0:16, j * D + c * 128: j * D + (c + 1) * 128], identf[0:16, 0:16], SCTb.all() + ["identf"], [pk(b)])
                copy("dve", STj[:], PS[b][:, 0:128].rearrange("p (c r) -> p c r", r=16), [pk(b)], ["ST%d" % j])

        def chk(stage):
            MARKS.append(("p%d_s%d" % (pas, stage), sum(1 for o in P.ops if o.eng == "pe")))
            if KSTOP != pas * 20 + stage:
                return
            YT = [Buf(K, "YTd%d" % i, 1, D, F32, arena_off=49152 + i * 4096) for i in range(2)]
            for tb in range(8):
                yt = YT[tb % 2]
                to_tok(lambda c, tb=tb: Xb.ap[:, c, tb * 128:(tb + 1) * 128], 128, yt.ap[:, 0, :], yt.k(0, 0), Xb.ks(range(8), tb * 128, 128))
                outs.append(dma("sp", yp[t0 + tb * 128:t0 + (tb + 1) * 128, :], yt.ap[:, 0, :], yt.all(), [], "o_yt%d" % (tb % 2)))
            if hasS:
                yt = YT[0]
                to_tok(lambda c: Xb.ap[:, c, TP:NT], NS, yt.ap[:, 0, :], yt.k(0, 0), Xb.ks(range(8), TP, NS))
                outs.append(dma("sp", ys, yt.ap[0:NS, 0, :], yt.all(), [], "o_yt0"))
            raise Stop()

        chk(1)

        def nxt(grow, name):
            return lambda si, part="ab": norm_sub(Xb, grow, Hb, subs, si, (pas, name), part)

        ffn(0, 1, subs, (pas, "f01"), nxt(2, "conv"))
        chk(2)
        rmsnorm(Xb, 2, Hb, subs, (pas, "conv"))
        BC = Buf(K, "BC", 8, NT, BF16, subs, arena_off=0)
        VBs = [Buf(K, "VB%d" % i, 1, NT + 2, F32, arena_off=17408 + i * 14720) for i in range(2)]
        CVs = [Buf(K, "CV%d" % i, 1, NT, F32, arena_off=17408 + i * 14720 + 4224) for i in range(2)]
        BBfs = [Buf(K, "BBf%d" % i, 1, NT, F32, arena_off=17408 + i * 14720 + 2 * 4224) for i in range(2)]
        UUs = [Buf(K, "UU%d" % i, 1, 512, F32, arena_off=17408 + i * 14720 + 3 * 4224) for i in range(2)]
        for c2 in range(4):
            parts2 = [wget(("ci", c2 * 2)), wget(("ci", c2 * 2 + 1))]
            for cc in range(2):
                c = c2 * 2 + cc
                parts = parts2[cc]
                VB, CV, BBf, UU = VBs[cc], CVs[cc], BBfs[cc], UUs[cc]
                copy("act", VB.ap[:, 0, 0:2], convst[:, c, :], ["convst%d" % c], VB.all())
                for si, (off, n) in enumerate(subs):
                    bs = [nb(), nb(), nb()]
                    for pi in range(3):
                        pt, pkey = parts[pi]
                        for k in range(8):
                            mm(PS[bs[pi]][:, 0:n], pt[:, k, 0:128], Hb.ap[:, k, off:off + n], k == 0, k == 7, [pkey] + Hb.ks(k, off, n), [pk(bs[pi])])
                    copy("act", UU.ap[:, 0, 0:n], PS[bs[2]][:, 0:n], [pk(bs[2])], UU.all())
                    tt(VB.ap[:, 0, 2 + off:2 + off + n], PS[bs[1]][:, 0:n], UU.ap[:, 0, 0:n], ALU.mult, [pk(bs[1])] + UU.all(), VB.all())
                    copy("act", BBf.ap[:, 0, off:off + n], PS[bs[0]][:, 0:n], [pk(bs[0])], BBf.all())
                P.op("dve", lambda e, c=c, CV=CV, VB=VB: e.tensor_scalar(out=CV.ap[:, 0, 0:TP], in0=VB.ap[:, 0, 2:2 + TP], scalar1=gc(13, c), scalar2=None, op0=ALU.mult),
                     VB.all() + ["VC"], CV.all())
                stt(CV.ap[:, 0, 0:TP], VB.ap[:, 0, 1:1 + TP], gc(12, c), CV.ap[:, 0, 0:TP], ALU.mult, ALU.add, VB.all() + CV.all() + ["VC"], CV.all())
                stt(CV.ap[:, 0, 0:TP], VB.ap[:, 0, 0:TP], gc(11, c), CV.ap[:, 0, 0:TP], ALU.mult, ALU.add, VB.all() + CV.all() + ["VC"], CV.all())
                if hasS:
                    P.op("dve", lambda e, c=c, CV=CV, VB=VB: e.tensor_scalar(out=CV.ap[:, 0, TP:NT], in0=VB.ap[:, 0, 2 + TP:2 + NT], scalar1=gc(13, c), scalar2=None, op0=ALU.mult),
                         VB.all() + ["VC"], CV.all())
                    stt(CV.ap[:, 0, TP:NT], ST1[:, c, :], gc(12, c), CV.ap[:, 0, TP:NT], ALU.mult, ALU.add, ["ST1", "VC"] + CV.all(), CV.all())
                    stt(CV.ap[:, 0, TP:NT], ST0[:, c, :], gc(11, c), CV.ap[:, 0, TP:NT], ALU.mult, ALU.add, ["ST0", "VC"] + CV.all(), CV.all())
                    copy("act", NV[:, c, :], VB.ap[:, 0, 2 + TP:2 + NT], VB.all(), ["NV"])
                for si, (off, n) in enumerate(subs):
                    tt(BC.ap[:, c, off:off + n], BBf.ap[:, 0, off:off + n], CV.ap[:, 0, off:off + n], ALU.mult, BBf.all() + CV.all(), BC.ks(c, off, n))
                copy("act", convst[:, c, :], VB.ap[:, 0, TP:TP + 2], VB.all(), ["convst%d" % c])
        col = [wget(("co", 0))[0], wget(("co", 1))[0]]
        proj_add(col, 8, BC, subs, 1.0, nxt(4, "xa0"))
        if hasS:
            CPT = Buf(K, "CPT", 1, D, F32, arena_off=0)
            to_tok(lambda c: convst[:, c, :], 2, CPT.ap[:, 0, :], CPT.k(0, 0), ["convst%d" % c for c in range(8)])
            outs.append(dma("sp", cpo, CPT.ap[0:2, 0, :], CPT.all(), [], "o_cpt"))
            NVT = Buf(K, "NVT", 1, D, F32, arena_off=4096)
            to_tok(lambda c: NV[:, c, :], NS, NVT.ap[:, 0, :], NVT.k(0, 0), ["NV"])
            outs.append(dma("sp", cso[:, 1, :], NVT.ap[0:NS, 0, :], NVT.all(), [], "o_nvt"))
            outs.append(dma("sp", cso[:, 0, :], sconv[:, 1, :], [], [], "o_cs0"))
        chk(3)
        xattn(0, subs, psubs, hasS, NT, (pas, "xa0"), nxt(8, "f02"))
        chk(4)
        ffn(0, 2, subs, (pas, "f02"), nxt(1, "f11"))
        chk(5)
        ffn(1, 1, subs, (pas, "f11"), nxt(3, "rec"))
        chk(6)
        rec(subs, psubs, hasS, NT, (pas, "rec"), nxt(5, "xa1"))
        chk(7)
        xattn(1, subs, psubs, hasS, NT, (pas, "xa1"), nxt(9, "f12"))
        chk(8)
        ffn(1, 2, subs, (pas, "f12"))
        chk(9)
        YF = Buf(K, "YF", 8, NT, F32, subs, arena_off=0)
        if pas == 0 and KSTOP == -1:
            prefetch_x(1)
        rmsnorm(Xb, 10, YF, subs)
        YT = [Buf(K, "YT%d" % i, 1, D, F32, arena_off=36864 + i * 4096) for i in range(2)]
        for tb in range(8):
            yt = YT[tb % 2]
            to_tok(lambda c, tb=tb: YF.ap[:, c, tb * 128:(tb + 1) * 128], 128, yt.ap[:, 0, :], yt.k(0, 0), YF.ks(range(8), tb * 128, 128))
            outs.append(dma("sp", yp[t0 + tb * 128:t0 + (tb + 1) * 128, :], yt.ap[:, 0, :], yt.all(), [], "o_yt%d" % (tb % 2)))
        if hasS:
            yt = YT[0]
            to_tok(lambda c: YF.ap[:, c, TP:NT], NS, yt.ap[:, 0, :], yt.k(0, 0), YF.ks(range(8), TP, NS))
            outs.append(dma("sp", ys, yt.ap[0:NS, 0, :], yt.all(), [], "o_yt0"))

    def xattn(l, subs, psubs, hasS, NT, ntag_, after_sub):
        rmsnorm(Xb, 4 + l, Hb, subs, ntag_)
        QX = Buf(K, "QX", 8, NT, BF16, subs, arena_off=0)
        PTs = [Buf(K, "PT%d" % i, 2, 512, BF16, arena_off=16640 + i * 2048) for i in range(2)]
        RDs_ = [Buf(K, "RD%d" % i, 1, 512, F32, arena_off=20736 + i * 2048) for i in range(2)]
        RDt = [Buf(K, "RDt%d" % i, 1, 512, F32, arena_off=24832 + i * 2048) for i in range(2)]
        KSb = [Buf(K, "KS%d" % i, 2, D, BF16, arena_off=28928 + i * 4096) for i in range(4)]
        VSb = [Buf(K, "VS%d" % i, 2, D, BF16, arena_off=45312 + i * 4096) for i in range(6)]
        PR = Buf(K, "PR", 1, D, F32, arena_off=69888)
        ql = [wget(("xq", l, 0))[0], wget(("xq", l, 1))[0]]
        for c in range(8):
            qt, qk = ql[c // 4]
            for si, (off, n) in enumerate(subs):
                b = nb()
                for k in range(8):
                    mm(PS[b][:, 0:n], qt[:, k, (c % 4) * 128:(c % 4 + 1) * 128], Hb.ap[:, k, off:off + n], k == 0, k == 7, [qk] + Hb.ks(k, off, n), [pk(b)])
                copy(cp_eng(), QX.ap[:, c, off:off + n], PS[b][:, 0:n], [pk(b)], QX.ks(c, off, n), scale=1.0 / 16)
        BO = 6
        bqr = [0]
        if hasS:
            for hf in range(2):
                b = nb()
                qt, qk = ql[hf]
                for k in range(8):
                    mm(PS[b][0:NS, :], Hb.ap[:, k, TP:NT], qt[:, k, :], k == 0, k == 7, [qk] + Hb.ks(k, TP, NS), [pk(b)])
                copy("act", QTK[:, hf * 512:(hf + 1) * 512], PS[b][0:NS, :], [pk(b)], ["QTK"], scale=1.0 / 16)

        def load(s):
            ks_, vs_ = KSb[s % 4], VSb[s % 6]
            dma("pool", ks_.ap, ck[l, s].rearrange("(c p) d -> p c d", p=128), [], ks_.all(), "ks%d" % (s % 4))
            dma("pool", vs_.ap, cv[l, s].rearrange("(c p) d -> p c d", p=128), [], vs_.all(), "vs%d" % (s % 6))

        def front(s):
            ks_ = KSb[s % 4]
            for hf in range(2):
                BQ = 3 + bqr[0] % 3
                bqr[0] += 1
                mm(PS[BQ][:], selb[0:NS, s, :], QTK[0:NS, hf * 512:(hf + 1) * 512], True, True, ["selb", "QTK"], [pk(BQ)])
                for nc_ in range(2):
                    tt(PR.ap[:, 0, nc_ * 512:(nc_ + 1) * 512], ks_.ap[:, nc_, hf * 512:(hf + 1) * 512], PS[BQ][:], ALU.mult, ks_.all() + [pk(BQ)], PR.all())
                P.op("dve", lambda e, hf=hf: e.tensor_reduce(out=SC[:].rearrange("p (n h) -> p n h", h=4)[:, :, 2 * hf:2 * hf + 2],
                                                              in_=PR.ap[:, 0, :].rearrange("p (n h d) -> p n h d", h=2, d=256),
                                                              axis=AX.X, op=ALU.add), PR.all(), ["SC"])
            copy("dve", SCs[:, s, :], SC[:], ["SC"], [("SCs", s)])

        def front_exp(s):
            act(ESs[:, s, :], SCs[:, s, :], AF.Exp, [("SCs", s)], [("ES", s)])

        def back(s):
            vs_ = VSb[s % 6]
            for c in range(8):
                for nc_ in range(2):
                    mm(PS[BO][:, c * NS + s: c * NS + s + 1], vs_.ap[:, nc_, c * 128:(c + 1) * 128], ESs[:, s, nc_ * 4 + c // 2: nc_ * 4 + c // 2 + 1],
                       nc_ == 0, nc_ == 1, vs_.all() + [("ES", s)], [pk(BO)])
            for nc_ in range(2):
                mm(PS[BO][:, 128 + s * 4:128 + (s + 1) * 4], ones1[:], ESs[:, s, nc_ * 4:(nc_ + 1) * 4], nc_ == 0, nc_ == 1, ["ones1", ("ES", s)], [pk(BO)])

        it = 0
        if hasS:
            load(0)
            load(1)
        for h in range(4):
            for si, (off, n) in enumerate(psubs):
                if hasS:
                    if it >= 2:
                        back(2 * it - 4)
                        back(2 * it - 3)
                    if it < 7:
                        load(2 * it + 2)
                        load(2 * it + 3)
                    front(2 * it)
                PT, RD, RT = PTs[it % 2], RDs_[it % 2], RDt[it % 2]
                if hasS:
                    bs_, bd = [0, 1], 2
                else:
                    bs_ = [0, 1] if it % 2 == 0 else [2, 3]
                    bd = 4
                for nc_ in range(2):
                    for dc in range(2):
                        mm(PS[bs_[nc_]][:, 0:n], KTb[:, l, h * 2 + dc, nc_ * 128:(nc_ + 1) * 128], QX.ap[:, h * 2 + dc, off:off + n], dc == 0, dc == 1,
                           [("KTb", l)] + QX.ks(h * 2 + dc, off, n), [pk(bs_[nc_])])
                    act(PT.ap[:, nc_, 0:n], PS[bs_[nc_]][:, 0:n], AF.Exp, [pk(bs_[nc_])], PT.ks(nc_))
                for nc_ in range(2):
                    mm(PS[bd][:, 0:n], ones1[:], PT.ap[:, nc_, 0:n], nc_ == 0, nc_ == 1, ["ones1"] + PT.ks(nc_), [pk(bd)])
                act(RT.ap[:, 0, 0:n], PS[bd][:, 0:n], AF.Ln, [pk(bd)], RT.all())
                act(RD.ap[:, 0, 0:n], RT.ap[:, 0, 0:n], AF.Exp, RT.all(), RD.all(), scale=-1.0)
                for dc in range(2):
                    bo = bs_[dc]
                    for nc_ in range(2):
                        mm(PS[bo][:, 0:n], Vbb[:, l, nc_, h * 256 + dc * 128: h * 256 + (dc + 1) * 128], PT.ap[:, nc_, 0:n], nc_ == 0, nc_ == 1,
                           [("Vbb", l)] + PT.ks(nc_), [pk(bo)])
                    tt(QX.ap[:, h * 2 + dc, off:off + n], PS[bo][:, 0:n], RD.ap[:, 0, 0:n], ALU.mult, [pk(bo)] + RD.all(), QX.ks(h * 2 + dc, off, n))
                if hasS:
                    front(2 * it + 1)
                    front_exp(2 * it)
                    front_exp(2 * it + 1)
                it += 1
        if hasS:
            for s_ in range(12, 16):
                back(s_)
            P.op("dve", lambda e: e.reciprocal(out=RDs[:], in_=PS[BO][:, 128:192].rearrange("p (s h) -> p s h", h=4)), [pk(BO)], ["RDs"])
            for h in range(4):
                tt(QX.ap[:, 2 * h:2 * h + 2, TP:NT], PS[BO][:, 2 * h * NS:(2 * h + 2) * NS].rearrange("p (c s) -> p c s", s=NS),
                   RDs[:, :, h].unsqueeze(1).to_broadcast([128, 2, NS]), ALU.mult, [pk(BO), "RDs"], QX.ks([2 * h, 2 * h + 1], TP, NS))
        ol = [wget(("xo", l, 0))[0], wget(("xo", l, 1))[0]]
        proj_add(ol, 8, QX, subs, 1.0, after_sub)

    def rec(subs, psubs, hasS, NT, ntag_, after_sub):
        rmsnorm(Xb, 3, Hb, subs, ntag_)
        A = 0
        SETS = []
        for q in range(2):
            st_ = []
            for i in range(5):
                st_.append(Buf(K, "SCR%d_%d" % (q, i), 1, 528, F32, arena_off=A)); A += 2112
            st_.append(Buf(K, "KHF%d" % q, 1, 512, BF16, arena_off=A)); A += 1024
            SETS.append(st_)
        QS, SG, LF, BBc, EB, KHF = SETS[0]
        QT = Buf(K, "QT", 4, NT, BF16, subs, arena_off=A); A += 8320
        KT2 = Buf(K, "KT2", 4, NT, BF16, subs, arena_off=A); A += 8320
        GS = Buf(K, "GS", 4, NT, BF16, subs, arena_off=A); A += 8320
        KHT = Buf(K, "KHT", 8, 512, BF16, arena_off=A); A += 8192
        VT = Buf(K, "VT", 8, 512, BF16, arena_off=A); A += 8192
        ATT = Buf(K, "ATT", 1, 256, BF16, arena_off=A); A += 512
        EBL = Buf(K, "EBL", 4, 16, F32, arena_off=A); A += 256
        SSb = [Buf(K, "SS%d" % i, 4, 128, F32, arena_off=A + i * 2048) for i in range(2)]; A += 4096
        SNb = [Buf(K, "SN%d" % i, 4, 128, F32, arena_off=A + i * 2048) for i in range(2)]; A += 4096
        assert A <= ARENA, A

        OSB = [SETS[0][0], SETS[0][1], SETS[1][0], SETS[1][1]]

        def ostage_a(hl, n, src, srckeys):
            copy(cp_eng(), OSB[hl].ap[:, 0, 0:n], src, srckeys, OSB[hl].all())

        def ostage_sq(hl, off, n):
            O2 = SETS[hl % 2][5]
            ob = OSB[hl]
            act(O2.ap[:, 0, 0:n], ob.ap[:, 0, 0:n], AF.Square, ob.all(), O2.all())

        def ostage_b(hl, off, n, do_sq=True):
            q = hl % 2
            A_, B_, C_, O2 = SETS[q][2], SETS[q][3], SETS[q][4], SETS[q][5]
            ob = OSB[hl]
            if do_sq:
                ostage_sq(hl, off, n)
            mm(PS[6][:, 0:n], onesH[:], O2.ap[:, 0, 0:n], True, True, ["onesH"] + O2.all(), [pk(6)])
            act(A_.ap[:, 0, 0:n], PS[6][:, 0:n], AF.Ln, [pk(6), "epsc"], A_.all(), bias=epsc[:])
            act(B_.ap[:, 0, 0:n], A_.ap[:, 0, 0:n], AF.Exp, A_.all(), B_.all(), scale=-0.5)
            tt(C_.ap[:, 0, 0:n], ob.ap[:, 0, 0:n], B_.ap[:, 0, 0:n], ALU.mult, ob.all() + B_.all(), C_.all())
            stt(GS.ap[:, hl, off:off + n], C_.ap[:, 0, 0:n], gon[:], GS.ap[:, hl, off:off + n], ALU.mult, ALU.mult,
                C_.all() + ["gon"] + GS.ks(hl, off, n), GS.ks(hl, off, n))

        pending = []

        for g in range(2):
            (vt_, vk_), = wget(("rv", g))
            for tb in range(8):
                b = nb()
                for k in range(8):
                    mm(PS[b][:], Hb.ap[:, k, tb * 128:(tb + 1) * 128], vt_[:, k, :], k == 0, k == 7, [vk_] + Hb.ks(k, tb * 128, 128), [pk(b)])
                copy(cp_eng(), VT.ap[:, tb, :], PS[b][:], [pk(b)], VT.ks(tb))
            if hasS:
                b = nb()
                for k in range(8):
                    mm(PS[b][0:NS, :], Hb.ap[:, k, TP:NT], vt_[:, k, :], k == 0, k == 7, [vk_] + Hb.ks(k, TP, NS), [pk(b)])
                copy("dve", IST[:, g * 512:(g + 1) * 512], PS[b][0:NS, :], [pk(b)], ["IST"])
            MARKS.append(("rec_g%d_v" % g, sum(1 for o in P.ops if o.eng == "pe")))
            tiles = [([0], 0, 512), ([1, 2], 512, 528)] if hasS else [([0], 0, 512), ([1], 512, 512)]

            def front(hl, h, parts, tile, S):
                QS, SG, LF, BBc, EB, KHF = S
                sis, toff, ntot = tile
                for si in sis:
                    off, n = subs[si]
                    co = off - toff
                    bs = [nb(), nb(), nb()]
                    for pi in range(3):
                        pt, pkey = parts[pi]
                        for k in range(8):
                            mm(PS[bs[pi]][:, 0:n], pt[:, k, 0:128], Hb.ap[:, k, off:off + n], k == 0, k == 7, [pkey] + Hb.ks(k, off, n), [pk(bs[pi])])
                    act(QS.ap[:, 0, co:co + n], PS[bs[0]][:, 0:n], AF.Silu, [pk(bs[0])], QS.all())
                    act(SG.ap[:, 0, co:co + n], PS[bs[1]][:, 0:n], AF.Tanh, [pk(bs[1])], SG.all(), scale=0.5)
                    act(GS.ap[:, hl, off:off + n], PS[bs[2]][:, 0:n], AF.Silu, [pk(bs[2])], GS.ks(hl, off, n))

            def front_b(hl, h, tile, S):
                QS, SG, LF, BBc, EB, KHF = S
                sis, off, n = tile
                P.op("act", lambda e: e.activation(out=LF.ap[:, 0, 0:n], in_=SG.ap[:, 0, 0:n], func=AF.Ln, bias=LBH[:, h:h + 1], scale=HOML[:, h:h + 1]),
                     SG.all() + ["LBH", "HOML"], LF.all())
                P.op("dve", lambda e: e.tensor_scalar(out=SG.ap[:, 0, 0:n], in0=SG.ap[:, 0, 0:n], scalar1=NHOML[:, h:h + 1], scalar2=HOML[:, h:h + 1],
                                                      op0=ALU.mult, op1=ALU.add), SG.all() + LF.all() + ["NHOML", "HOML"], SG.all())
                P.op("dve", lambda e: e.tensor_tensor_scan(out=BBc.ap[:, 0, 0:n], data0=SM[:, off:off + n], data1=LF.ap[:, 0, 0:n], initial=0.0,
                                                           op0=ALU.mult, op1=ALU.add), ["SM"] + LF.all(), BBc.all())

            def back(hl, h, tile, S):
                QS, SG, LF, BBc, EB, KHF = S
                sis, off, n = tile
                si = sis[0]
                act(EB.ap[:, 0, 0:n], BBc.ap[:, 0, 0:n], AF.Exp, BBc.all(), EB.all())
                act(LF.ap[:, 0, 0:n], BBc.ap[:, 0, 0:n], AF.Exp, BBc.all(), LF.all(), scale=-1.0)
                tt(QT.ap[:, hl, off:off + n], QS.ap[:, 0, 0:n], EB.ap[:, 0, 0:n], ALU.mult, QS.all() + EB.all(), QT.ks(hl, off, n))
                tt(LF.ap[:, 0, 0:n], SG.ap[:, 0, 0:n], LF.ap[:, 0, 0:n], ALU.mult, SG.all() + LF.all(), LF.all())
                copy("pool", KT2.ap[:, hl, off:off + n], LF.ap[:, 0, 0:n], LF.all(), KT2.ks(hl, off, n))
                tt(KHF.ap[:, 0, :].rearrange("p (c t) -> p c t", t=64), LF.ap[:, 0, 0:512].rearrange("p (c t) -> p c t", t=64),
                   EB.ap[:, 0, 0:512].rearrange("p (c t) -> p c t", t=64)[:, :, 63:64].to_broadcast([128, 8, 64]), ALU.mult,
                   LF.all() + EB.all(), KHF.all())
                copy("pool", EBL.ap[:, hl, si * 8:(si + 1) * 8], EB.ap[:, 0, 0:512].rearrange("p (c t) -> p c t", t=64)[:, :, 63], EB.all(), EBL.ks(hl))
                if len(sis) == 2:
                    copy("dve", KKs[:, h, :], SG.ap[:, 0, 512:528], SG.all(), [("KKs", h)])
                    copy("dve", QSs[:, h, :], QS.ap[:, 0, 512:528], QS.all(), [("QSs", h)])
                    copy("pool", Fs[:, h, :], EB.ap[:, 0, 512:528], EB.all(), [("Fs", h)])

            def back_tr(hl, h, tile, S):
                QS, SG, LF, BBc, EB, KHF = S
                sis, off, n = tile
                si = sis[0]
                for tbl in range(4):
                    tr(PSB[:, tbl * 128:(tbl + 1) * 128], KHF.ap[:, 0, tbl * 128:(tbl + 1) * 128], identb[:], KHF.all() + ["identb"], ["psb"])
                copy("dve", KHT.ap[:, si * 4:(si + 1) * 4, hl * 128:(hl + 1) * 128], PSB[:, 0:512].rearrange("p (c t) -> p c t", t=128),
                     ["psb"], KHT.ks(range(si * 4, si * 4 + 4)))

            for hl in range(4):
                h = g * 4 + hl
                parts = wget(("rq", h))
                front(hl, h, parts, tiles[0], SETS[0])
                front(hl, h, parts, tiles[1], SETS[1])
                if hl > 0:
                    back_tr(hl - 1, h - 1, tiles[0], SETS[0])
                    back_tr(hl - 1, h - 1, tiles[1], SETS[1])
                front_b(hl, h, tiles[0], SETS[0])
                front_b(hl, h, tiles[1], SETS[1])
                back(hl, h, tiles[0], SETS[0])
                back(hl, h, tiles[1], SETS[1])
            back_tr(3, g * 4 + 3, tiles[0], SETS[0])
            back_tr(3, g * 4 + 3, tiles[1], SETS[1])
            MARKS.append(("rec_g%d_st1" % g, sum(1 for o in P.ops if o.eng == "pe")))
            def sample_step(s, g=g):
                ss, sn = SSb[s % 2], SNb[s % 2]
                dma("sp", ss.ap, srec[s, g * 4:(g + 1) * 4].rearrange("h k v -> k h v"), [], ss.all(), "ss%d" % (s % 2))
                mm(PS[6][:], selb[0:NS, s, :], IST[0:NS, g * 512:(g + 1) * 512], True, True, ["selb", "IST"], [pk(6)])
                for hl in range(4):
                    h = g * 4 + hl
                    P.op("dve", lambda e, hl=hl, h=h, s=s, ss=ss: e.tensor_scalar(out=ss.ap[:, hl, :], in0=ss.ap[:, hl, :], scalar1=Fs[:, h, s:s + 1], scalar2=None, op0=ALU.mult),
                         ss.ks(hl) + [("Fs", h)], ss.ks(hl))
                    stt(sn.ap[:, hl, :], PS[6][:, hl * 128:(hl + 1) * 128], KKs[:, h, s:s + 1], ss.ap[:, hl, :], ALU.mult, ALU.add,
                        [pk(6), ("KKs", h)] + ss.ks(hl), sn.ks(hl))
                for hl in range(4):
                    h = g * 4 + hl
                    mm(PS[4][:, 256 + hl * NS + s: 256 + hl * NS + s + 1], sn.ap[:, hl, :], QSs[:, h, s:s + 1], True, True, sn.ks(hl) + [("QSs", h)], [pk(4)])
                outs.append(dma("sp", rso[s, g * 4:(g + 1) * 4].rearrange("h k v -> k h v"), sn.ap, sn.all(), [], "rs%d" % (s % 2)))

            for tb in range(8):
                for hf in range(2):
                    ch = 2 * tb + hf
                    for hl in range(4):
                        mm(PS[4][hf * 64:(hf + 1) * 64, hl * 64:(hl + 1) * 64], KT2.ap[:, hl, ch * 64:(ch + 1) * 64], QT.ap[:, hl, ch * 64:(ch + 1) * 64], True, True,
                           KT2.ks(hl, ch * 64, 64) + QT.ks(hl, ch * 64, 64), [pk(4)])
                tt(ATT.ap[:, 0, :].rearrange("p (h t) -> p h t", t=64), PS[4][:, 0:256].rearrange("p (h t) -> p h t", t=64),
                   cm[:].unsqueeze(1).to_broadcast([128, 4, 64]), ALU.mult, [pk(4), "cm"], ATT.all())
                for hf in range(2):
                    ch = 2 * tb + hf
                    pb = hf * 64
                    si, cc = ch // 8, ch % 8
                    for hl in range(4):
                        bd_ = 5 + hl // 2
                        mm(PS[bd_][:, (hl % 2) * 128:(hl % 2 + 1) * 128], KHT.ap[pb:pb + 64, tb, hl * 128:(hl + 1) * 128], VT.ap[pb:pb + 64, tb, hl * 128:(hl + 1) * 128], True, True,
                           KHT.ks(tb) + VT.ks(tb), [pk(bd_)])
                    for hl in range(4):
                        h = g * 4 + hl
                        mm(PS[hl][:, cc * 64:(cc + 1) * 64], VT.ap[pb:pb + 64, tb, hl * 128:(hl + 1) * 128], ATT.ap[pb:pb + 64, 0, hl * 64:(hl + 1) * 64], True, False,
                           VT.ks(tb) + ATT.all(), [pk(hl)])
                        mm(PS[hl][:, cc * 64:(cc + 1) * 64], Sbf[:, h, :], QT.ap[:, hl, ch * 64:(ch + 1) * 64], False, True,
                           ["Sbh%d" % h] + QT.ks(hl, ch * 64, 64), [pk(hl)])
                    for hl in range(4):
                        h = g * 4 + hl
                        bd_ = 5 + hl // 2
                        stt(Sf[:, h, :], Sf[:, h, :], EBL.ap[:, hl, ch:ch + 1], PS[bd_][:, (hl % 2) * 128:(hl % 2 + 1) * 128], ALU.mult, ALU.add,
                            ["Sf%d" % h, pk(bd_)] + EBL.ks(hl), ["Sf%d" % h])
                        copy("act", Sbf[:, h, :], Sf[:, h, :], ["Sf%d" % h], ["Sbh%d" % h])
                    if hasS and os.environ.get("KINT", "1") == "1":
                        sample_step(ch)
                    if pending:
                        if cc == 1:
                            ostage_sq(*pending[0]); ostage_sq(*pending[1])
                        elif cc == 2:
                            ostage_b(*pending[0], do_sq=False); ostage_b(*pending[1], do_sq=False)
                            ostage_sq(*pending[2]); ostage_sq(*pending[3])
                        elif cc == 3:
                            ostage_b(*pending[2], do_sq=False); ostage_b(*pending[3], do_sq=False)
                            del pending[:]
                    if cc == 7:
                        off, n = psubs[si]
                        for hl in range(4):
                            ostage_a(hl, n, PS[hl][:, 0:n], [pk(hl)])
                            pending.append((hl, off, n))
            for (hl_, off_, n_) in pending:
                ostage_b(hl_, off_, n_)
            del pending[:]
            if hasS:
                outs.append(dma("sp", rpo[g * 4:(g + 1) * 4].rearrange("h k v -> k h v"), Sf[:, g * 4:(g + 1) * 4, :],
                                ["Sf%d" % (g * 4 + i) for i in range(4)], [], "o_rp"))
                if os.environ.get("KINT", "1") != "1":
                    for s_ in range(NS):
                        sample_step(s_)
                off, n = subs[2]
                for hl in range(4):
                    ostage_a(hl, n, PS[4][:, 256 + hl * NS:256 + (hl + 1) * NS], [pk(4)])
                    ostage_b(hl, off, n)
            MARKS.append(("rec_g%d_st2" % g, sum(1 for o in P.ops if o.eng == "pe")))
            proj_add([wget(("ro", g))[0]], 4, GS, subs, 1.0, after_sub if g == 1 else None)

    try:
        if KSTOP != 0:
            run_pass(0)
            if KSTOP != 10:
                run_pass(1)
                assert wpos[0] == len(sched), (wpos, len(sched))
    except Stop:
        pass
    print("kernel build: ops=%d" % len(P.ops))
    if os.environ.get("KMARKS"):
        import json
        MARKS.append(("end", sum(1 for o in P.ops if o.eng == "pe")))
        json.dump(MARKS, open(os.environ["KMARKS"], "w"))
    P.emit(final_wait_ops=outs)
    return nc


_CACHE = {}


def kernel(**inp):
    if "nc" not in _CACHE:
        _CACHE["nc"] = build_program()
    nc = _CACHE["nc"]
    f = lambda a: np.ascontiguousarray(np.asarray(a, dtype=np.float32))
    wnames = ["norm_ffn1", "w_ffn1_gate", "w_ffn1_up", "w_ffn1_down", "norm_mix", "w_conv_in", "w_conv", "w_conv_out", "lb_raw",
              "w_rec_in", "g_rec_onorm", "w_rec_out", "norm_xattn", "norm_mem", "w_xq", "w_xkv", "w_xo", "norm_ffn2",
              "w_ffn2_gate", "w_ffn2_up", "w_ffn2_down", "norm_final"]
    shared = {n: f(inp[n]) for n in wnames}
    in_maps = []
    for b in range(8):
        m = dict(shared)
        sl = slice(NS * b, NS * (b + 1))
        m["xp"] = f(inp["x_prompt"][b])
        m["xs"] = f(inp["x_sample"][sl, 0])
        m["mem"] = f(inp["mem_prompt"][b])
        m["sconv"] = f(inp["state_conv"][0, sl])
        m["srec"] = f(inp["state_rec"][0, sl])
        m["ck"] = f(inp["cache_mem_k"][:, sl].reshape(2, NS, 256, D))
        m["cv"] = f(inp["cache_mem_v"][:, sl].reshape(2, NS, 256, D))
        in_maps.append(m)
    res = run_bass_kernel_spmd(nc, in_maps, core_ids=list(range(8)))
    R = res.results
    y_prompt = np.stack([R[b]["yp"] for b in range(8)])
    y_sample = np.concatenate([R[b]["ys"] for b in range(8)])[:, None, :]
    mk_ = np.stack([R[b]["mk"] for b in range(8)], axis=1).reshape(2, 8, 256, 4, 256)
    mv_ = np.stack([R[b]["mv"] for b in range(8)], axis=1).reshape(2, 8, 256, 4, 256)
    conv_p = np.stack([R[b]["cp"] for b in range(8)])[None]
    rec_p = np.stack([R[b]["rp"] for b in range(8)])[None]
    conv_s = np.concatenate([R[b]["cs"] for b in range(8)])[None]
    rec_s = np.concatenate([R[b]["rs"] for b in range(8)])[None]
    return (y_prompt.astype(np.float32), y_sample.astype(np.float32), mk_.astype(np.float32), mv_.astype(np.float32),
            conv_p.astype(np.float32), rec_p.astype(np.float32), conv_s.astype(np.float32), rec_s.astype(np.float32))
```
